# Optimizing a Trainium2 kernel written in Bass

```python
import math
import jax
import jax.numpy as jnp
from jax import lax
import numpy as np

D_MODEL = 1024
BATCH = 32
SEQ = 2048
DEPTH = 4

GRID_W = 64
CTX_LEN = 256
N_MIXERS = 4
HEAD_DIM = 64
ATTN_SCALE = HEAD_DIM ** -0.5
ROPE_THETA = 10000.0
NORM_EPS = 1e-6
Q_BLOCK = 128
N_MOD = 6
DA_HEADS = D_MODEL // (2 * HEAD_DIM)
NA_HEADS = D_MODEL // HEAD_DIM
NA_ROWS_MAX = 8
NA_COLS = 16
LRU_WIDTH = D_MODEL
LRU_BLOCK = 256
LRU_BLOCKS = LRU_WIDTH // LRU_BLOCK
LRU_CONV = 4
LRU_C = 8.0
GQA_Q_HEADS = D_MODEL // HEAD_DIM
GQA_KV_HEADS = GQA_Q_HEADS // 4
D_FF = 4 * D_MODEL

kernel_name = 'hybrid_interleaved_diffusion_block'


def n_uses(kind):
    return len(range(kind, DEPTH, N_MIXERS))


def rms_norm(x, g):
    xf = x.astype(jnp.float32)
    y = xf * lax.rsqrt(jnp.mean(xf * xf, axis=-1, keepdims=True) + NORM_EPS)
    return (y * g.astype(jnp.float32)).astype(x.dtype)


def modulate(h, shift, scale):
    return h * (1 + scale) + shift


def squared_relu_mlp(h, w1, w2):
    return jnp.square(jax.nn.relu(h @ w1)) @ w2


def axial_rope_table(n_tok):
    t = jnp.arange(n_tok)
    row = (t // GRID_W).astype(jnp.float32)
    col = (t % GRID_W).astype(jnp.float32)
    n_freq = HEAD_DIM // 4
    inv = ROPE_THETA ** (-jnp.arange(n_freq, dtype=jnp.float32) / n_freq)
    ang = jnp.concatenate([row[:, None] * inv, col[:, None] * inv], axis=-1)
    return jnp.cos(ang), jnp.sin(ang)


def apply_axial_rope(x, cos, sin):
    T = x.shape[1]
    half = x.shape[-1] // 2
    shp = (1, T) + (1,) * (x.ndim - 3) + (half,)
    cs = cos.reshape(shp)
    sn = sin.reshape(shp)
    xf = x.astype(jnp.float32).reshape(x.shape[:-1] + (half, 2))
    x0 = xf[..., 0]
    x1 = xf[..., 1]
    out = jnp.stack([x0 * cs - x1 * sn, x0 * sn + x1 * cs], axis=-1)
    return out.reshape(x.shape).astype(x.dtype)


def sweep_query_blocks(fn, q, k, v):
    B, S = q.shape[:2]
    nb = S // Q_BLOCK
    qb = q.reshape((B, nb, Q_BLOCK) + q.shape[2:]).swapaxes(0, 1)
    ob = lax.map(lambda qq: fn(qq, k, v), qb)
    return ob.swapaxes(0, 1).reshape((B, S) + ob.shape[3:])


def gqa_attend(q, k, v):
    B, Tq, Hq, dh = q.shape
    Hkv = k.shape[2]
    qg = q.reshape(B, Tq, Hkv, Hq // Hkv, dh)
    s = jnp.einsum('bqhgd,bkhd->bhgqk', qg, k).astype(jnp.float32) * ATTN_SCALE
    p = jax.nn.softmax(s, axis=-1).astype(v.dtype)
    o = jnp.einsum('bhgqk,bkhd->bqhgd', p, v)
    return o.reshape(B, Tq, Hq, dh)


def diff_attend(q, k, v, lam):
    s = jnp.einsum('bqhmd,bkhmd->bhmqk', q, k).astype(jnp.float32) * ATTN_SCALE
    p = jax.nn.softmax(s, axis=-1)
    w = (p[:, :, 0] - lam * p[:, :, 1]).astype(v.dtype)
    return jnp.einsum('bhqk,bkhe->bqhe', w, v)


def diff_attention_mixer(h_lat, h_ctx, p, layer_idx, cos, sin, need_ctx):
    w_qkv, q_gain, k_gain, lq1, lk1, lq2, lk2, subln_g, w_o = p
    lam_init = 0.8 - 0.6 * math.exp(-0.3 * layer_idx)
    lam = (jnp.exp(jnp.sum(lq1.astype(jnp.float32) * lk1.astype(jnp.float32)))
           - jnp.exp(jnp.sum(lq2.astype(jnp.float32) * lk2.astype(jnp.float32))) + lam_init)

    def proj_q(h):
        B, T, _ = h.shape
        q = (h @ w_qkv[:, :D_MODEL]).reshape(B, T, DA_HEADS, 2, HEAD_DIM)
        return rms_norm(q, q_gain)

    def proj_kv(h):
        B, T, _ = h.shape
        k, v = jnp.split(h @ w_qkv[:, D_MODEL:], 2, axis=-1)
        k = rms_norm(k.reshape(B, T, DA_HEADS, 2, HEAD_DIM), k_gain)
        return k, v.reshape(B, T, DA_HEADS, 2 * HEAD_DIM)

    def finish(o):
        B, T = o.shape[:2]
        o = rms_norm(o, subln_g) * (1.0 - lam_init)
        return o.reshape(B, T, D_MODEL) @ w_o

    q_l = apply_axial_rope(proj_q(h_lat), cos, sin)
    k_l, v_l = proj_kv(h_lat)
    k_l = apply_axial_rope(k_l, cos, sin)
    k_c, v_c = proj_kv(h_ctx)
    k_all = jnp.concatenate([k_c, k_l], axis=1)
    v_all = jnp.concatenate([v_c, v_l], axis=1)
    o_l = sweep_query_blocks(lambda qb, kb, vb: diff_attend(qb, kb, vb, lam), q_l, k_all, v_all)
    y_l = finish(o_l)
    y_c = finish(diff_attend(proj_q(h_ctx), k_c, v_c, lam)) if need_ctx else None
    return y_l, y_c


def heads_q(h, w_q, n_heads, gain):
    B, T, _ = h.shape
    return rms_norm((h @ w_q).reshape(B, T, n_heads, HEAD_DIM), gain)


def heads_kv(h, w_kv, n_heads, gain):
    B, T, _ = h.shape
    k, v = jnp.split(h @ w_kv, 2, axis=-1)
    k = rms_norm(k.reshape(B, T, n_heads, HEAD_DIM), gain)
    return k, v.reshape(B, T, n_heads, HEAD_DIM)


def neighbourhood_mixer(h_lat, h_ctx, p, need_ctx):
    w_qkv, q_gain, k_gain, rpb, w_o = p
    B, S, _ = h_lat.shape
    rows = S // GRID_W
    kr = min(NA_ROWS_MAX, rows)
    n_loc = kr * GRID_W
    w_q = w_qkv[:, :D_MODEL]
    w_kv = w_qkv[:, D_MODEL:]
    q_l = heads_q(h_lat, w_q, NA_HEADS, q_gain)
    k_l, v_l = heads_kv(h_lat, w_kv, NA_HEADS, k_gain)
    k_c, v_c = heads_kv(h_ctx, w_kv, NA_HEADS, k_gain)
    k_grid = k_l.reshape(B, rows, GRID_W, NA_HEADS, HEAD_DIM)
    v_grid = v_l.reshape(B, rows, GRID_W, NA_HEADS, HEAD_DIM)
    q_rows = q_l.reshape(B, rows, GRID_W, NA_HEADS, HEAD_DIM).swapaxes(0, 1)
    r_idx = jnp.arange(rows)
    row_start = jnp.clip(r_idx - kr // 2, 0, rows - kr)
    col = jnp.arange(GRID_W)
    col_start = jnp.clip(col - NA_COLS // 2, 0, GRID_W - NA_COLS)
    col_in = (col[None, :] >= col_start[:, None]) & (col[None, :] < col_start[:, None] + NA_COLS)
    col_bias_idx = jnp.clip(col[None, :] - col[:, None], -(NA_COLS - 1), NA_COLS - 1) + NA_COLS - 1

    def row_block(args):
        q_row, r, r0 = args
        k_band = lax.dynamic_slice_in_dim(k_grid, r0, kr, axis=1)
        v_band = lax.dynamic_slice_in_dim(v_grid, r0, kr, axis=1)
        s_loc = jnp.einsum('bqhd,brkhd->bhqrk', q_row, k_band).astype(jnp.float32) * ATTN_SCALE
        row_off = r0 + jnp.arange(kr) - r + NA_ROWS_MAX - 1
        bias = rpb[:, row_off][:, :, col_bias_idx].transpose(0, 2, 1, 3)
        s_loc = jnp.where(col_in[None, None, :, None, :], s_loc + bias[None].astype(jnp.float32), -jnp.inf)
        s_ctx = jnp.einsum('bqhd,bkhd->bhqk', q_row, k_c).astype(jnp.float32) * ATTN_SCALE
        s_all = jnp.concatenate([s_loc.reshape(B, NA_HEADS, GRID_W, n_loc), s_ctx], axis=-1)
        prob = jax.nn.softmax(s_all, axis=-1).astype(v_band.dtype)
        p_loc = prob[..., :n_loc].reshape(B, NA_HEADS, GRID_W, kr, GRID_W)
        o = (jnp.einsum('bhqrk,brkhd->bqhd', p_loc, v_band)
             + jnp.einsum('bhqk,bkhd->bqhd', prob[..., n_loc:], v_c))
        return o.reshape(B, GRID_W, NA_HEADS * HEAD_DIM)

    o = lax.map(row_block, (q_rows, r_idx, row_start))
    y_l = o.swapaxes(0, 1).reshape(B, S, D_MODEL) @ w_o
    y_c = None
    if need_ctx:
        q_c = heads_q(h_ctx, w_q, NA_HEADS, q_gain)
        y_c = gqa_attend(q_c, k_c, v_c).reshape(B, h_ctx.shape[1], D_MODEL) @ w_o
    return y_l, y_c


def centred_depthwise_conv(u, w, b):
    T = u.shape[1]
    left = LRU_CONV // 2
    right = LRU_CONV - 1 - left
    up = jnp.pad(u, ((0, 0), (left, right), (0, 0)))
    out = b + up[:, 0:T] * w[0]
    for j in range(1, LRU_CONV):
        out = out + up[:, j:j + T] * w[j]
    return out


def rglru_coeffs(u, w_a, b_a, w_x, b_x, lam):
    B, T, _ = u.shape
    ub = u.reshape(B, T, LRU_BLOCKS, LRU_BLOCK)
    r = jax.nn.sigmoid((jnp.einsum('btnd,nde->btne', ub, w_a).reshape(B, T, LRU_WIDTH) + b_a).astype(jnp.float32))
    i = jax.nn.sigmoid((jnp.einsum('btnd,nde->btne', ub, w_x).reshape(B, T, LRU_WIDTH) + b_x).astype(jnp.float32))
    log_a = -LRU_C * r * jax.nn.softplus(-lam.astype(jnp.float32))
    a = jnp.exp(log_a)
    b = jnp.sqrt(-jnp.expm1(2.0 * log_a)) * (i * u.astype(jnp.float32))
    return a, b


def scan_combine(left, right):
    a1, b1 = left
    a2, b2 = right
    return a1 * a2, a2 * b1 + b2


def linear_scan(a, b, h0, reverse):
    if reverse:
        b = b.at[:, -1].add(a[:, -1] * h0)
    else:
        b = b.at[:, 0].add(a[:, 0] * h0)
    _, h = lax.associative_scan(scan_combine, (a, b), reverse=reverse, axis=1)
    return h


def rglru_mixer(h_lat, h_ctx, p, need_ctx):
    (w_in, conv_w, conv_b, f_wa, f_ba, f_wx, f_bx, f_lam,
     b_wa, b_ba, b_wx, b_bx, b_lam, w_o) = p
    w_gate = w_in[:, :LRU_WIDTH]
    w_rec = w_in[:, LRU_WIDTH:]
    u_l = centred_depthwise_conv(h_lat @ w_rec, conv_w, conv_b)
    u_c = centred_depthwise_conv(h_ctx @ w_rec, conv_w, conv_b)
    h0 = jnp.zeros((h_ctx.shape[0], LRU_WIDTH), jnp.float32)
    a, b = rglru_coeffs(u_c, f_wa, f_ba, f_wx, f_bx, f_lam)
    hc_f = linear_scan(a, b, h0, False)
    a, b = rglru_coeffs(u_c, b_wa, b_ba, b_wx, b_bx, b_lam)
    hc_b = linear_scan(a, b, h0, True)
    a, b = rglru_coeffs(u_l, f_wa, f_ba, f_wx, f_bx, f_lam)
    hl_f = linear_scan(a, b, hc_f[:, -1], False)
    a, b = rglru_coeffs(u_l, b_wa, b_ba, b_wx, b_bx, b_lam)
    hl_b = linear_scan(a, b, hc_b[:, 0], True)

    def out(hf, hb, h):
        gate = jax.nn.gelu(h @ w_gate)
        return ((hf + hb).astype(gate.dtype) * gate) @ w_o

    y_l = out(hl_f, hl_b, h_lat)
    y_c = out(hc_f, hc_b, h_ctx) if need_ctx else None
    return y_l, y_c


def gqa_mixer(h_lat, h_ctx, p, cos, sin, need_ctx):
    w_qkv, q_gain, k_gain, w_o = p
    B, S, _ = h_lat.shape
    nq = GQA_Q_HEADS * HEAD_DIM
    w_q = w_qkv[:, :nq]
    w_kv = w_qkv[:, nq:]
    q_l = apply_axial_rope(heads_q(h_lat, w_q, GQA_Q_HEADS, q_gain), cos, sin)
    k_l, v_l = heads_kv(h_lat, w_kv, GQA_KV_HEADS, k_gain)
    k_l = apply_axial_rope(k_l, cos, sin)
    k_c, v_c = heads_kv(h_ctx, w_kv, GQA_KV_HEADS, k_gain)
    k_all = jnp.concatenate([k_c, k_l], axis=1)
    v_all = jnp.concatenate([v_c, v_l], axis=1)
    o_l = sweep_query_blocks(gqa_attend, q_l, k_all, v_all)
    y_l = o_l.reshape(B, S, D_MODEL) @ w_o
    y_c = None
    if need_ctx:
        q_c = heads_q(h_ctx, w_q, GQA_Q_HEADS, q_gain)
        y_c = gqa_attend(q_c, k_c, v_c).reshape(B, h_ctx.shape[1], D_MODEL) @ w_o
    return y_l, y_c


def setup_inputs(seed: int = 0) -> dict:
    key = jax.random.key(seed)
    ks = iter(jax.random.split(key, 64))
    D = D_MODEL

    def normal(shape, scale):
        return jax.random.normal(next(ks), shape, jnp.float32) * scale

    def gain(shape):
        return 1.0 + normal(shape, 0.02)

    def lru_lambda(n):
        a0 = jax.random.uniform(next(ks), (n, LRU_WIDTH), jnp.float32, minval=0.9, maxval=0.999)
        s = a0 ** (1.0 / LRU_C)
        return jnp.log(s) - jnp.log1p(-s)

    na, nb, nc, nd = n_uses(0), n_uses(1), n_uses(2), n_uses(3)
    return {
        'x': normal((BATCH, SEQ, D), 1.0),
        'c': normal((BATCH, D), 1.0),
        'ctx': normal((BATCH, CTX_LEN, D), 1.0),
        'c_ctx': normal((D,), 1.0),
        'norm1_g': gain((DEPTH, D)),
        'norm2_g': gain((DEPTH, D)),
        'w_mod': normal((DEPTH, D, N_MOD * D), 0.5 * D ** -0.5),
        'b_mod': normal((DEPTH, N_MOD * D), 0.02),
        'w_mlp1': normal((DEPTH, D, D_FF), D ** -0.5),
        'w_mlp2': normal((DEPTH, D_FF, D), D_FF ** -0.5),
        'a_w_qkv': normal((na, D, 3 * D), D ** -0.5),
        'a_q_norm_g': gain((na, HEAD_DIM)),
        'a_k_norm_g': gain((na, HEAD_DIM)),
        'a_lambda_q1': normal((na, HEAD_DIM), 0.1),
        'a_lambda_k1': normal((na, HEAD_DIM), 0.1),
        'a_lambda_q2': normal((na, HEAD_DIM), 0.1),
        'a_lambda_k2': normal((na, HEAD_DIM), 0.1),
        'a_subln_g': gain((na, 2 * HEAD_DIM)),
        'a_w_o': normal((na, D, D), D ** -0.5),
        'b_w_qkv': normal((nb, D, 3 * D), D ** -0.5),
        'b_q_norm_g': gain((nb, HEAD_DIM)),
        'b_k_norm_g': gain((nb, HEAD_DIM)),
        'b_rpb': normal((nb, NA_HEADS, 2 * NA_ROWS_MAX - 1, 2 * NA_COLS - 1), 0.02),
        'b_w_o': normal((nb, D, D), D ** -0.5),
        'c_w_in': normal((nc, D, 2 * LRU_WIDTH), D ** -0.5),
        'c_conv_w': normal((nc, LRU_CONV, LRU_WIDTH), LRU_CONV ** -0.5),
        'c_conv_b': normal((nc, LRU_WIDTH), 0.01),
        'c_fwd_w_a': normal((nc, LRU_BLOCKS, LRU_BLOCK, LRU_BLOCK), LRU_BLOCK ** -0.5),
        'c_fwd_b_a': normal((nc, LRU_WIDTH), 0.01),
        'c_fwd_w_x': normal((nc, LRU_BLOCKS, LRU_BLOCK, LRU_BLOCK), LRU_BLOCK ** -0.5),
        'c_fwd_b_x': normal((nc, LRU_WIDTH), 0.01),
        'c_fwd_lam': lru_lambda(nc),
        'c_bwd_w_a': normal((nc, LRU_BLOCKS, LRU_BLOCK, LRU_BLOCK), LRU_BLOCK ** -0.5),
        'c_bwd_b_a': normal((nc, LRU_WIDTH), 0.01),
        'c_bwd_w_x': normal((nc, LRU_BLOCKS, LRU_BLOCK, LRU_BLOCK), LRU_BLOCK ** -0.5),
        'c_bwd_b_x': normal((nc, LRU_WIDTH), 0.01),
        'c_bwd_lam': lru_lambda(nc),
        'c_w_o': normal((nc, LRU_WIDTH, D), LRU_WIDTH ** -0.5),
        'd_w_qkv': normal((nd, D, (GQA_Q_HEADS + 2 * GQA_KV_HEADS) * HEAD_DIM), D ** -0.5),
        'd_q_norm_g': gain((nd, HEAD_DIM)),
        'd_k_norm_g': gain((nd, HEAD_DIM)),
        'd_w_o': normal((nd, D, D), D ** -0.5),
    }


def reference(x, c, ctx, c_ctx, norm1_g, norm2_g, w_mod, b_mod, w_mlp1, w_mlp2,
              a_w_qkv, a_q_norm_g, a_k_norm_g, a_lambda_q1, a_lambda_k1, a_lambda_q2, a_lambda_k2,
              a_subln_g, a_w_o,
              b_w_qkv, b_q_norm_g, b_k_norm_g, b_rpb, b_w_o,
              c_w_in, c_conv_w, c_conv_b, c_fwd_w_a, c_fwd_b_a, c_fwd_w_x, c_fwd_b_x, c_fwd_lam,
              c_bwd_w_a, c_bwd_b_a, c_bwd_w_x, c_bwd_b_x, c_bwd_lam, c_w_o,
              d_w_qkv, d_q_norm_g, d_k_norm_g, d_w_o):
    S = x.shape[1]
    cos, sin = axial_rope_table(S)
    mixer_params = (
        (a_w_qkv, a_q_norm_g, a_k_norm_g, a_lambda_q1, a_lambda_k1, a_lambda_q2, a_lambda_k2, a_subln_g, a_w_o),
        (b_w_qkv, b_q_norm_g, b_k_norm_g, b_rpb, b_w_o),
        (c_w_in, c_conv_w, c_conv_b, c_fwd_w_a, c_fwd_b_a, c_fwd_w_x, c_fwd_b_x, c_fwd_lam,
         c_bwd_w_a, c_bwd_b_a, c_bwd_w_x, c_bwd_b_x, c_bwd_lam, c_w_o),
        (d_w_qkv, d_q_norm_g, d_k_norm_g, d_w_o),
    )
    for i in range(DEPTH):
        kind = i % N_MIXERS
        p = tuple(arr[i // N_MIXERS] for arr in mixer_params[kind])
        need_ctx = i < DEPTH - 1
        mod_l = jnp.split((jax.nn.silu(c) @ w_mod[i] + b_mod[i])[:, None, :], N_MOD, axis=-1)
        mod_c = jnp.split((jax.nn.silu(c_ctx) @ w_mod[i] + b_mod[i])[None, None, :], N_MOD, axis=-1)
        h_l = modulate(rms_norm(x, norm1_g[i]), mod_l[0], mod_l[1])
        h_c = modulate(rms_norm(ctx, norm1_g[i]), mod_c[0], mod_c[1])
        if kind == 0:
            y_l, y_c = diff_attention_mixer(h_l, h_c, p, i, cos, sin, need_ctx)
        elif kind == 1:
            y_l, y_c = neighbourhood_mixer(h_l, h_c, p, need_ctx)
        elif kind == 2:
            y_l, y_c = rglru_mixer(h_l, h_c, p, need_ctx)
        else:
            y_l, y_c = gqa_mixer(h_l, h_c, p, cos, sin, need_ctx)
        x = x + mod_l[2] * y_l
        x = x + mod_l[5] * squared_relu_mlp(modulate(rms_norm(x, norm2_g[i]), mod_l[3], mod_l[4]),
                                            w_mlp1[i], w_mlp2[i])
        if need_ctx:
            ctx = ctx + mod_c[2] * y_c
            ctx = ctx + mod_c[5] * squared_relu_mlp(modulate(rms_norm(ctx, norm2_g[i]), mod_c[3], mod_c[4]),
                                                    w_mlp1[i], w_mlp2[i])
    return x
```

```python
import math
import contextlib
import numpy as np
import concourse.bass as bass
import concourse.mybir as mybir
from concourse.bass_utils import run_bass_kernel_spmd

F32 = mybir.dt.float32
BF16 = mybir.dt.bfloat16
AF = mybir.ActivationFunctionType
ALU = mybir.AluOpType
AX = mybir.AxisListType

D = 1024
NCTX = 256
NLAT = 2048
T = NCTX + NLAT
NKT = T // 128
EPS = 1e-6
SCALE = 0.125
ENGINES = ("pe", "act", "dve", "pool", "sp")


class Buf:
    __slots__ = ("name", "last_w", "readers", "dsem_id")

    def __init__(self, name, last_w=None):
        self.name = name
        self.last_w = last_w
        self.readers = []
        self.dsem_id = None


class Op:
    __slots__ = ("eng", "fn", "reads", "writes", "idx", "eidx", "deps", "signal",
                 "tick", "is_dma", "dsem", "waits")

    def __init__(self, eng, fn, reads, writes, is_dma=False, dsem=None):
        self.eng = eng
        self.fn = fn
        self.reads = reads
        self.writes = writes
        self.is_dma = is_dma
        self.dsem = dsem
        self.signal = False
        self.tick = None
        self.deps = []
        self.waits = []


class Prog:
    def __init__(self, nc):
        self.nc = nc
        self.ops = []
        self.deferred = []
        self.n_dsem = 0

    def op(self, eng, fn, reads=(), writes=()):
        o = Op(eng, fn, tuple(reads), tuple(writes))
        self.ops.append(o)
        return o

    def _mkdma(self, eng, fn, reads, writes, sem_buf):
        if sem_buf.dsem_id is None:
            sem_buf.dsem_id = self.n_dsem
            self.n_dsem += 1
        return Op(eng, fn, tuple(reads), tuple(writes), is_dma=True, dsem=sem_buf.dsem_id)

    def dma(self, eng, fn, reads, writes, sem_buf):
        o = self._mkdma(eng, fn, reads, writes, sem_buf)
        self.ops.append(o)
        return o

    def dma_at(self, pos, eng, fn, reads, writes, sem_buf):
        o = self._mkdma(eng, fn, reads, writes, sem_buf)
        self.deferred.append((pos, len(self.deferred), o))
        return o

    def pos(self):
        return len(self.ops)

    def finalize(self):
        if self.deferred:
            self.deferred.sort(key=lambda x: (x[0], x[1]))
            out = []
            di = 0
            nd = len(self.deferred)
            for i, o in enumerate(self.ops):
                while di < nd and self.deferred[di][0] <= i:
                    out.append(self.deferred[di][2])
                    di += 1
                out.append(o)
            while di < nd:
                out.append(self.deferred[di][2])
                di += 1
            self.ops = out
            self.deferred = []
        self.eng_ops = {e: [] for e in ENGINES}
        for i, o in enumerate(self.ops):
            o.idx = i
            o.eidx = len(self.eng_ops[o.eng])
            self.eng_ops[o.eng].append(o)

    def resolve(self):
        self.finalize()
        for o in self.ops:
            deps = {}
            for b in o.reads:
                if b.last_w is not None:
                    deps[b.last_w.idx] = b.last_w
            for b in o.writes:
                if b.last_w is not None:
                    deps[b.last_w.idx] = b.last_w
                for r in b.readers:
                    deps[r.idx] = r
            for b in o.writes:
                b.last_w = o
                b.readers = []
            for b in o.reads:
                if b.last_w is not o:
                    b.readers.append(o)
            deps.pop(o.idx, None)
            o.deps = list(deps.values())
        dma_count = {}
        dma_pos = {}
        for o in self.ops:
            if o.is_dma:
                dma_count[o.dsem] = dma_count.get(o.dsem, 0) + 1
                dma_pos[o.idx] = dma_count[o.dsem]
        waited = {e: {} for e in ENGINES}
        for o in self.ops:
            need = {}
            for d in o.deps:
                if d.is_dma:
                    key = ("d", d.dsem)
                    pos = dma_pos[d.idx]
                else:
                    if d.eng == o.eng and not o.is_dma:
                        if o.eng == "pe":
                            continue
                        if o.eidx - d.eidx > 2:
                            continue
                    key = ("e", d.eng)
                    pos = d.eidx
                if key not in need or need[key][0] < pos:
                    need[key] = (pos, d)
            w = waited[o.eng]
            for key, (pos, d) in need.items():
                if w.get(key, -1) >= pos:
                    continue
                w[key] = pos
                d.signal = True
                o.waits.append(d)
        ecount = {e: 0 for e in ENGINES}
        dcount = {}
        for o in self.ops:
            if o.is_dma:
                dcount[o.dsem] = dcount.get(o.dsem, 0) + 16
                o.tick = dcount[o.dsem]
            elif o.signal:
                ecount[o.eng] += 1
                o.tick = ecount[o.eng]

    def emit(self, final_waits=()):
        nc = self.nc
        self.resolve()
        with contextlib.ExitStack() as st:
            esem = {e: st.enter_context(nc.semaphore(f"s_{e}")) for e in ENGINES}
            dsem = [st.enter_context(nc.semaphore(f"d_{i}")) for i in range(self.n_dsem)]
            block = st.enter_context(nc.Block())

            def run(eng_name, eng):
                for o in self.eng_ops[eng_name]:
                    for d in o.waits:
                        if d.is_dma:
                            eng.wait_ge(dsem[d.dsem], d.tick)
                        else:
                            eng.wait_ge(esem[d.eng], d.tick)
                    ins = o.fn(eng)
                    if o.is_dma:
                        ins.then_inc(dsem[o.dsem], 16)
                    elif o.signal:
                        ins.then_inc(esem[o.eng], 1)
                if eng_name == "sp":
                    seen = {}
                    for d in final_waits:
                        seen[d.dsem] = max(seen.get(d.dsem, 0), d.tick)
                    for k, v in seen.items():
                        eng.wait_ge(dsem[k], v)

            @block.tensor
            def _(e):
                run("pe", e)

            @block.scalar
            def _(e):
                run("act", e)

            @block.vector
            def _(e):
                run("dve", e)

            @block.gpsimd
            def _(e):
                run("pool", e)

            @block.sync
            def _(e):
                run("sp", e)


class Tile:
    __slots__ = ("t", "b")

    def __init__(self, t, b):
        self.t = t
        self.b = b


def _pvec(v):
    return np.ascontiguousarray(np.asarray(v, np.float32).reshape(8, 128).T)


def _gvec(g, swap=False):
    idx = np.arange(128) % 64
    if swap:
        idx = idx ^ 1
    return np.asarray(g, np.float32)[idx].reshape(128, 1)


def _bc(v):
    v = np.asarray(v, np.float32).reshape(1, -1)
    return np.repeat(v, 128, axis=0)


def rope_tables():
    t = np.arange(NLAT)
    row = (t // 64).astype(np.float32)
    col = (t % 64).astype(np.float32)
    inv = (np.float32(10000.0) ** (-np.arange(16, dtype=np.float32) / np.float32(16))).astype(np.float32)
    ang = np.concatenate([row[:, None] * inv, col[:, None] * inv], axis=-1).astype(np.float32)
    cos = np.cos(ang).astype(np.float32)
    sin = np.sin(ang).astype(np.float32)
    p = np.arange(128)
    i = (p % 64) // 2
    sign = np.where(p % 2 == 0, -1.0, 1.0).astype(np.float32)
    C = np.ascontiguousarray(cos[:, i].T)
    S = np.ascontiguousarray(sin[:, i].T * sign[:, None])
    return C.astype(np.float32), S.astype(np.float32)


def _r0(r):
    return min(max(r - 4, 0), 24)


def na_plan():
    types = {}
    order = []
    plan = []

    def get_type(dt_, pat):
        key = (dt_, pat)
        if key not in types:
            types[key] = len(order)
            order.append(key)
        return types[key]

    r = 8
    for kb in range(r - 4, r + 6, 2):
        pat = tuple(tuple(1 if (_r0(r + qp) <= kb + kp < _r0(r + qp) + 8) else 0 for qp in (0, 1)) for kp in (0, 1))
        get_type(kb - r, pat)
    for g in range(16):
        r = 2 * g
        lst = []
        for kb in range(0, 32, 2):
            pat = tuple(tuple(1 if (_r0(r + qp) <= kb + kp < _r0(r + qp) + 8) else 0 for qp in (0, 1)) for kp in (0, 1))
            if not any(any(x) for x in pat):
                continue
            lst.append((kb // 2, get_type(kb - r, pat)))
        plan.append(lst)
    return plan, order


def na_tables(rpb):
    plan, order = na_plan()
    nt = len(order)
    rpb = np.asarray(rpb, np.float32)
    kp = np.arange(128) // 64
    kc = np.arange(128) % 64
    qp = np.arange(128) // 64
    qc = np.arange(128) % 64
    c0 = np.clip(qc - 8, 0, 48)
    colin = (kc[:, None] >= c0[None, :]) & (kc[:, None] < c0[None, :] + 16)
    cidx = np.clip(kc[:, None] - qc[None, :], -15, 15) + 15
    bias = np.zeros((16, 128, nt, 128), np.float32)
    mask = np.zeros((128, nt, 128), np.float32)
    for ti, (dt_, pat) in enumerate(order):
        patm = np.asarray(pat, np.float32)
        valid = patm[kp[:, None], qp[None, :]] * colin
        ridx = np.clip(dt_ + kp[:, None] - qp[None, :] + 7, 0, 14)
        bias[:, :, ti, :] = rpb[:, ridx, cidx]
        mask[:, ti, :] = valid
    return plan, nt, bias, mask


PP_LAYOUT = None


def build_pp(inp):
    items = []

    def add(name, arr):
        arr = np.asarray(arr, np.float32).reshape(128, -1)
        items.append((name, arr))

    add("n1g", np.stack([_pvec(inp["norm1_g"][i]) for i in range(4)], axis=1))
    add("n2g", np.stack([_pvec(inp["norm2_g"][i]) for i in range(4)], axis=1))
    add("bmod", np.stack([inp["b_mod"][i].reshape(6, 8, 128).transpose(2, 0, 1).reshape(128, 48) for i in range(4)], axis=1))
    for pre, qn, kn in (("a", "a_q_norm_g", "a_k_norm_g"), ("b", "b_q_norm_g", "b_k_norm_g"), ("d", "d_q_norm_g", "d_k_norm_g")):
        add(pre + "_qg", _gvec(inp[qn][0]))
        add(pre + "_qgs", _gvec(inp[qn][0], True))
        add(pre + "_kg", _gvec(inp[kn][0]))
        add(pre + "_kgs", _gvec(inp[kn][0], True))
    add("a_lam", np.concatenate([_bc(inp[n][0]) for n in ("a_lambda_q1", "a_lambda_k1", "a_lambda_q2", "a_lambda_k2")], axis=1))
    add("a_subln", _bc(inp["a_subln_g"][0]))
    add("c_convw", inp["c_conv_w"][0].reshape(4, 8, 128).transpose(2, 0, 1).reshape(128, 32))
    add("c_convb", _pvec(inp["c_conv_b"][0]))
    for n in ("c_fwd_b_a", "c_fwd_b_x", "c_fwd_lam", "c_bwd_b_a", "c_bwd_b_x", "c_bwd_lam"):
        add(n, _pvec(inp[n][0]))
    lay = {}
    off = 0
    for n, a in items:
        lay[n] = (off, a.shape[1])
        off += a.shape[1]
    return np.ascontiguousarray(np.concatenate([a for _, a in items], axis=1)), lay


def consts_arr():
    p = np.arange(128)
    ident = np.eye(128, dtype=np.float32)
    bones = (p[:, None] // 64 == p[None, :] // 64).astype(np.float32)
    swap = (p[:, None] == (p[None, :] ^ 1)).astype(np.float32)
    ones = np.ones((128, 128), np.float32)
    return np.ascontiguousarray(np.stack([ident, bones, swap, ones], axis=1))


class Builder:
    def __init__(self, nseq, layers, pp_lay, na_nt, na_plan_):
        self.nseq = nseq
        self.layers = layers
        self.lay = pp_lay
        self.na_nt = na_nt
        self.na_plan = na_plan_
        self.nc = bass.Bass("TRN2", target_bir_lowering=False)
        self.P = Prog(self.nc)
        self.gst = contextlib.ExitStack()
        self.barrier_op = None
        self.scope_bufs = None
        self.uid = 0

    def newbuf(self, name):
        b = Buf(name, self.barrier_op)
        if self.scope_bufs is not None:
            self.scope_bufs.append(b)
        return b

    def sb(self, name, shape, dt, st=None):
        self.uid += 1
        t = (st or self.gst).enter_context(self.nc.sbuf_tensor(f"{name}_{self.uid}", list(shape), dt))
        return Tile(t, self.newbuf(name))

    @contextlib.contextmanager
    def scope(self):
        st = contextlib.ExitStack()
        old = self.scope_bufs
        self.scope_bufs = []
        self.cur_st = st
        try:
            yield st
        finally:
            bufs = self.scope_bufs
            d = self.dummy
            self.barrier_op = self.P.op("pool", lambda e, a=d.t[:, 0:1]: e.memset(a, 0.0), bufs, bufs + [d.b])
            self.scope_bufs = old
            st.close()

    def dram_in(self, name, shape, dt=F32):
        return self.nc.dram_tensor(name, list(shape), dt, kind="ExternalInput").ap()

    def op(self, eng, fn, R, W):
        return self.P.op(eng, fn, [x.b if isinstance(x, Tile) else x for x in R],
                         [x.b if isinstance(x, Tile) else x for x in W])

    def mm(self, out_ap, out_t, lhsT, rhs, R, start, stop):
        return self.op("pe", lambda e, o=out_ap, l=lhsT, r=rhs, s=start, p=stop:
                       e.matmul(o, lhsT=l, rhs=r, start=s, stop=p), R, [out_t])

    def act(self, out, in_, func, R, W, bias=None, scale=None, accum_out=None):
        kw = {}
        if bias is not None:
            kw["bias"] = bias
        if scale is not None:
            kw["scale"] = scale
        if accum_out is not None:
            kw["accum_out"] = accum_out
        return self.op("act", lambda e, o=out, i=in_, f=func, kw=kw: e.activation(out=o, in_=i, func=f, **kw), R, W)

    def tt(self, eng, out, in0, in1, op, R, W):
        return self.op(eng, lambda e, o=out, a=in0, b=in1, p=op: e.tensor_tensor(out=o, in0=a, in1=b, op=p), R, W)

    def ts(self, eng, out, in0, s1, s2, op0, op1, R, W):
        if s2 is None:
            return self.op(eng, lambda e, o=out, a=in0, x=s1, p0=op0:
                           e.tensor_scalar(out=o, in0=a, scalar1=x, scalar2=None, op0=p0), R, W)
        return self.op(eng, lambda e, o=out, a=in0, x=s1, y=s2, p0=op0, p1=op1:
                       e.tensor_scalar(out=o, in0=a, scalar1=x, scalar2=y, op0=p0, op1=p1), R, W)

    def stt(self, out, in0, scalar, in1, op0, op1, R, W):
        return self.op("dve", lambda e, o=out, a=in0, s=scalar, b=in1, p0=op0, p1=op1:
                       e.scalar_tensor_tensor(out=o, in0=a, scalar=s, in1=b, op0=p0, op1=p1), R, W)

    def dma(self, eng, out, in_, R, W, sem):
        return self.P.dma(eng, lambda e, o=out, i=in_: e.dma_start(out=o, in_=i),
                          [x.b if isinstance(x, Tile) else x for x in R],
                          [x.b if isinstance(x, Tile) else x for x in W], sem.b if isinstance(sem, Tile) else sem)

    def ps(self):
        i = self.ps_rot
        self.ps_rot = (self.ps_rot + 1) % 4
        return self.pb[i]

    def acc(self):
        i = self.acc_rot
        self.acc_rot = (self.acc_rot + 1) % 4
        return self.pb[4 + i]

    NSLOT = 5
    PD = 1

    def wget(self, key, fn):
        r = self.ring
        if key in r["valid"]:
            return self.wslots[r["valid"][key]]
        n = r["n"]
        r["n"] += 1
        j = n + self.PD - self.NSLOT
        if j >= 0 and r["keys"][j] in r["valid"] and r["valid"][r["keys"][j]] == j % self.NSLOT:
            del r["valid"][r["keys"][j]]
        r["keys"].append(key)
        r["pos"].append(self.P.pos())
        slot = n % self.NSLOT
        for k2 in [k for k, v in r["valid"].items() if v == slot]:
            del r["valid"][k2]
        ipos = r["pos"][n - self.PD] if n >= self.PD else 0
        w = self.wslots[slot]
        for (o, i) in fn(w.t):
            self.P.dma_at(ipos, "pool", lambda e, o=o, i=i: e.dma_start(out=o, in_=i), [], [w.b], w.b)
        r["valid"][key] = slot
        return w

    def build(self):
        nc = self.nc
        nseq = self.nseq
        L = self.lay
        self.xT = self.dram_in("xT", [nseq, D, NLAT])
        self.cxT = self.dram_in("cxT", [nseq, D, NCTX])
        self.scT_d = self.dram_in("scT", [128, 8, 8])
        npp = sum(w for _, w in L.values())
        self.pp_d = self.dram_in("pp", [128, npp])
        self.cst_d = self.dram_in("cst", [128, 4, 128])
        self.ropeC_d = self.dram_in("ropeC", [128, NLAT])
        self.ropeS_d = self.dram_in("ropeS", [128, NLAT])
        self.w = {}
        for n, shp in (("w_mod", [4, D, 6 * D]), ("w_mlp1", [4, D, 4 * D]), ("w_mlp2", [4, 4 * D, D]),
                       ("a_w_qkv", [D, 3 * D]), ("a_w_o", [D, D]), ("b_w_qkv", [D, 3 * D]), ("b_w_o", [D, D]),
                       ("c_w_in", [D, 2 * D]), ("c_gw", [4, 4, 256, 256]), ("c_w_o", [D, D]),
                       ("d_w_qkv", [D, 1536]), ("d_w_o", [D, D])):
            self.w[n] = self.dram_in(n, shp)
        self.nab_d = self.dram_in("nabias", [16, 128, self.na_nt, 128])
        self.nam_d = self.dram_in("namask", [128, self.na_nt, 128])
        self.outT = nc.dram_tensor("outT", [nseq, D, NLAT], F32, kind="ExternalOutput").ap()
        self.cxS = nc.dram_tensor("cxS", [nseq, D, NCTX], F32, kind="Internal").ap()
        self.naE = nc.dram_tensor("naE", [16, 128, self.na_nt * 128], BF16, kind="Internal").ap()
        self.zS = nc.dram_tensor("zS", [nseq, D, T], BF16, kind="Internal").ap()
        self.DzS = [Buf(f"dz{s}") for s in range(nseq)]
        self.Dx = [[Buf(f"dx{s}_{b}") for b in range(5)] for s in range(nseq)]
        self.DnaE = Buf("naE")
        self.x_in_out = [False] * nseq
        self.c_in_s = [False] * nseq

        self.dummy = self.sb("dummy", [128, 8], F32)
        self.pp = self.sb("pp", [128, npp], F32)
        self.cst = self.sb("cst", [128, 4, 128], BF16)
        self.eps = self.sb("eps", [128, 1], F32)
        self.scS = self.sb("scS", [128, 8, 8], F32)
        self.modT = self.sb("modT", [128, 4, 48, 8], F32)
        self.xb = self.sb("xb", [128, 8, 512], F32)
        self.hb = self.sb("hb", [128, 8, 512], BF16)
        self.sqc = [self.sb(f"sqc{i}", [128, 512], BF16) for i in range(4)]
        self.std = self.sb("std", [128, 512], F32)
        self.rstd = self.sb("rstd", [128, 512], F32)
        self.tmpf = [self.sb(f"tmpf{i}", [128, 512], F32) for i in range(3)]
        self.xu = [self.sb(f"xu{i}", [128, 512], F32) for i in range(3)]
        self.small = [self.sb(f"small{i}", [128, 16], F32) for i in range(4)]
        self.wslots = [self.sb(f"wslot{i}", [128, 8, 512], BF16) for i in range(self.NSLOT)]
        self.ring = {"valid": {}, "n": 0, "keys": [], "pos": []}
        self.pb = []
        for i in range(8):
            t = self.gst.enter_context(nc.psum_tensor(f"pb{i}", [128, 512], F32))
            self.pb.append(Tile(t, self.newbuf(f"pb{i}")))
        self.ps_rot = 0
        self.acc_rot = 0
        self.rot = {"sqc": 0, "tmpf": 0, "xu": 0, "small": 0}
        self.stores = []

        self.dma("sp", self.pp.t[:], self.pp_d[:, :], [], [self.pp], self.pp)
        self.dma("pool", self.cst.t[:], self.cst_d[:, :, :], [], [self.cst], self.cst)
        self.dma("sp", self.scS.t[:], self.scT_d[:, :, :], [], [self.scS], self.scS)
        self.op("pool", lambda e: e.memset(self.eps.t[:], EPS), [], [self.eps])
        self.op("pool", lambda e: e.memset(self.dummy.t[:], 0.0), [], [self.dummy])
        self.ident = self.cst.t[:, 0, :]
        self.bones = self.cst.t[:, 1, :]
        self.swapm = self.cst.t[:, 2, :]
        self.ones = self.cst.t[:, 3, :]

        self.preamble_mod()
        if 1 in self.layers:
            self.preamble_na()
        for li in self.layers:
            for s in range(nseq):
                if li == 0:
                    self.attn_layer(li, s, "a")
                elif li == 1:
                    self.attn_layer(li, s, "b")
                elif li == 2:
                    self.lru_layer(li, s)
                else:
                    self.attn_layer(li, s, "d")
                self.mlp(li, s)
        self.P.emit(self.stores)
        return nc

    def ppv(self, name, a=None, b=None):
        off, w = self.lay[name]
        if a is None:
            return self.pp.t[:, off:off + w]
        return self.pp.t[:, off + a:off + b]

    def nxt(self, name):
        lst = getattr(self, name)
        i = self.rot[name]
        self.rot[name] = (i + 1) % len(lst)
        return lst[i]

    def preamble_mod(self):
        self.act(self.scS.t[:], self.scS.t[:], AF.Silu, [self.scS], [self.scS])
        with self.scope() as st:
            wm = [self.sb(f"wm{i}", [128, 8, 768], F32, st) for i in range(2)]
            for i in range(4):
                if i not in self.layers:
                    continue
                bank = self.ps()
                for cb in range(8):
                    wt = wm[cb % 2]
                    src = self.w["w_mod"][i, :, cb * 768:(cb + 1) * 768].rearrange("(kc p) n -> p kc n", p=128)
                    self.dma("sp", wt.t[:, 0:4, :], src[:, 0:4, :], [], [wt], wt)
                    self.dma("sp", wt.t[:, 4:8, :], src[:, 4:8, :], [], [wt], wt)
                    for m in range(6):
                        mm_ = cb * 6 + m
                        for kc in range(8):
                            self.mm(bank.t[:, mm_ * 8:mm_ * 8 + 8], bank, wt.t[:, kc, m * 128:(m + 1) * 128],
                                    self.scS.t[:, kc, :], [wt, self.scS], kc == 0, kc == 7)
                bo, _ = self.lay["bmod"]
                bm = self.pp.t[:, bo + i * 48: bo + (i + 1) * 48].unsqueeze(2).broadcast_to([128, 48, 8])
                self.tt("dve", self.modT.t[:, i, :, :], bank.t[:, 0:384].rearrange("p (m r) -> p m r", r=8), bm,
                        ALU.add, [bank, self.pp], [self.modT])
                for (j, gname) in ((1, "n1g"), (4, "n2g")):
                    go, _ = self.lay[gname]
                    g = self.pp.t[:, go + i * 8: go + (i + 1) * 8].unsqueeze(2).broadcast_to([128, 8, 8])
                    sl = self.modT.t[:, i, j * 8:(j + 1) * 8, :]
                    self.ts("dve", sl, sl, 1.0, None, ALU.add, None, [self.modT], [self.modT])
                    self.tt("dve", sl, sl, g, ALU.mult, [self.modT, self.pp], [self.modT])

    def mv(self, li, j, c, row):
        return self.modT.t[:, li, j * 8 + c, row:row + 1]

    def xsrc(self, s, blk):
        if blk == 0:
            base = self.cxS if self.c_in_s[s] else self.cxT
            return base[s].rearrange("(c p) t -> p c t", p=128)
        base = self.outT if self.x_in_out[s] else self.xT
        return base[s, :, (blk - 1) * 512: blk * 512].rearrange("(c p) t -> p c t", p=128)

    def xdst(self, s, blk):
        if blk == 0:
            return self.cxS[s].rearrange("(c p) t -> p c t", p=128)
        return self.outT[s, :, (blk - 1) * 512: blk * 512].rearrange("(c p) t -> p c t", p=128)

    @staticmethod
    def blk_n(blk):
        return 256 if blk == 0 else 512

    @staticmethod
    def blk_t0(blk):
        return 0 if blk == 0 else 256 + (blk - 1) * 512

    def load_x(self, s, blk):
        n = self.blk_n(blk)
        src = self.xsrc(s, blk)
        self.dma("sp", self.xb.t[:, 0:4, :n], src[:, 0:4, :], [self.Dx[s][blk]], [self.xb], self.xb)
        self.dma("sp", self.xb.t[:, 4:8, :n], src[:, 4:8, :], [self.Dx[s][blk]], [self.xb], self.xb)

    def norm_mod(self, s, blk, li, jB, jA, out_t, out_ap_fn):
        n = self.blk_n(blk)
        row = 4 if blk == 0 else s
        bank = self.ps()
        for c in range(8):
            sq = self.nxt("sqc")
            self.tt("pool", sq.t[:, :n], self.xb.t[:, c, :n], self.xb.t[:, c, :n], ALU.mult, [self.xb], [sq])
            self.mm(bank.t[:, :n], bank, self.ones, sq.t[:, :n], [self.cst, sq], c == 0, c == 7)
        self.act(self.std.t[:, :n], bank.t[:, :n], AF.Sqrt, [bank, self.eps], [self.std], bias=self.eps.t[:, 0:1], scale=1.0 / D)
        self.op("dve", lambda e, o=self.rstd.t[:, :n], i=self.std.t[:, :n]: e.reciprocal(out=o, in_=i), [self.std], [self.rstd])
        for c in range(8):
            tf = self.nxt("tmpf")
            self.stt(tf.t[:, :n], self.xb.t[:, c, :n], self.mv(li, jA, c, row), self.rstd.t[:, :n], ALU.mult, ALU.mult,
                     [self.xb, self.modT, self.rstd], [tf])
            self.act(out_ap_fn(c), tf.t[:, :n], AF.Identity, [tf, self.modT], [out_t], bias=self.mv(li, jB, c, row), scale=1.0)

    def update_x(self, s, blk, co, c0, nn, bank, li, jG):
        row = 4 if blk == 0 else s
        xu = self.nxt("xu")
        src = self.xsrc(s, blk)[:, co, c0:c0 + nn]
        dst = self.xdst(s, blk)[:, co, c0:c0 + nn]
        self.dma("sp", xu.t[:, :nn], src, [self.Dx[s][blk]], [xu], xu)
        self.stt(xu.t[:, :nn], bank.t[:, :nn], self.mv(li, jG, co, row), xu.t[:, :nn], ALU.mult, ALU.add,
                 [bank, self.modT, xu], [xu])
        st = self.dma("sp", dst, xu.t[:, :nn], [xu], [self.Dx2[s][blk]], xu)
        self.stores.append(st)

    def wtile(self, wname, idx, col0, ncols=512, krows=D, k0=0):
        w = self.w[wname] if idx is None else self.w[wname][idx]
        src = w[k0:k0 + 1024, col0:col0 + ncols].rearrange("(kc p) n -> p kc n", p=128)

        def fn(t, src=src, ncols=ncols):
            return [(t[:, 0:4, 0:ncols], src[:, 0:4, :]), (t[:, 4:8, 0:ncols], src[:, 4:8, :])]
        return self.wget((wname, idx, col0, ncols, k0), fn)

    def mlp(self, li, s):
        need_ctx = li < 3
        self.Dx2 = self.Dx
        with self.scope() as st:
            h1 = self.sb("h1", [128, 32, 512], BF16, st)
            for blk in range(0 if need_ctx else 1, 5):
                n = self.blk_n(blk)
                self.load_x(s, blk)
                self.norm_mod(s, blk, li, 3, 4, self.hb, lambda c, n=n: self.hb.t[:, c, :n])
                for j in range(32):
                    wt = self.wtile("w_mlp1", li, (j // 4) * 512)
                    bank = self.ps()
                    for kc in range(8):
                        self.mm(bank.t[:, :n], bank, wt.t[:, kc, (j % 4) * 128:(j % 4 + 1) * 128], self.hb.t[:, kc, :n],
                                [wt, self.hb], kc == 0, kc == 7)
                    tf = self.nxt("tmpf")
                    self.act(tf.t[:, :n], bank.t[:, :n], AF.Relu, [bank], [tf])
                    self.tt("pool", h1.t[:, j, :n], tf.t[:, :n], tf.t[:, :n], ALU.mult, [tf], [h1])
                for nb in range(2):
                    accs = [self.acc() for _ in range(4)]
                    for g in range(4):
                        wt = self.wtile("w_mlp2", li, nb * 512, k0=g * 1024)
                        for q in range(4):
                            for kk in range(8):
                                self.mm(accs[q].t[:, :n], accs[q], wt.t[:, kk, q * 128:(q + 1) * 128], h1.t[:, g * 8 + kk, :n],
                                        [wt, h1], g == 0 and kk == 0, g == 3 and kk == 7)
                    for q in range(4):
                        self.update_x(s, blk, nb * 4 + q, 0, n, accs[q], li, 5)
                if blk == 0:
                    self.c_in_s[s] = True
            self.x_in_out[s] = True

    def qk_post(self, bank, n, gname, gsname, rope_c0, out_ap, out_t, rope):
        qsb = self.nxt("sqc")
        self.act(qsb.t[:, :n], bank.t[:, :n], AF.Copy, [bank], [qsb])
        sq = self.nxt("sqc")
        self.act(sq.t[:, :n], bank.t[:, :n], AF.Square, [bank], [sq])
        bss = self.ps()
        self.mm(bss.t[:, :n], bss, self.bones, sq.t[:, :n], [self.cst, sq], True, True)
        std = self.nxt("tmpf")
        self.act(std.t[:, :n], bss.t[:, :n], AF.Sqrt, [bss, self.eps], [std], bias=self.eps.t[:, 0:1], scale=1.0 / 64)
        self.op("dve", lambda e, o=std.t[:, :n]: e.reciprocal(out=o, in_=o), [std], [std])
        g = self.ppv(gname)
        if not rope:
            self.stt(out_ap, bank.t[:, :n], g[:, 0:1], std.t[:, :n], ALU.mult, ALU.mult, [bank, self.pp, std], [out_t])
            return
        gs = self.ppv(gsname)
        brot = self.ps()
        self.mm(brot.t[:, :n], brot, self.swapm, qsb.t[:, :n], [self.cst, qsb], True, True)
        t1 = self.nxt("tmpf")
        self.stt(t1.t[:, :n], bank.t[:, :n], g[:, 0:1], self.ropeC.t[:, rope_c0:rope_c0 + n], ALU.mult, ALU.mult,
                 [bank, self.pp, self.ropeC], [t1])
        t2 = self.nxt("tmpf")
        self.stt(t2.t[:, :n], brot.t[:, :n], gs[:, 0:1], self.ropeS.t[:, rope_c0:rope_c0 + n], ALU.mult, ALU.mult,
                 [brot, self.pp, self.ropeS], [t2])
        self.tt("pool", t1.t[:, :n], t1.t[:, :n], t2.t[:, :n], ALU.add, [t1, t2], [t1])
        self.tt("dve", out_ap, t1.t[:, :n], std.t[:, :n], ALU.mult, [t1, std], [out_t])

    def attn_layer(self, li, s, kind):
        need_ctx = li < 3
        pre = kind
        wq = pre + "_w_qkv"
        wo = pre + "_w_o"
        rope = kind in ("a", "d")
        if kind == "d":
            nkc = 4
            nvh, dv = 4, 64
            kcol0 = 1024
            vcol0 = 1280
            nvcols = 256
        else:
            nkc = 8
            kcol0 = 1024
            vcol0 = 2048
            nvcols = 1024
            nvh, dv = (8, 128) if kind == "a" else (16, 64)
        dv1 = dv + 1
        self.Dx2 = self.Dx
        with self.scope() as st:
            KT = self.sb("KT", [128, nkc, T], BF16, st)
            VA = self.sb("VA", [128, NKT, nvh, dv1], BF16, st)
            QT = self.sb("QT", [128, 8, 512], BF16, st)
            otm = self.sb("otm", [128, 4, D], BF16, st)
            PTn = 896 if kind == "b" else 512
            PT = [self.sb(f"PT{i}", [128, PTn], BF16, st) for i in range(3)]
            ptr = [0]
            if rope:
                self.ropeC = self.sb("ropeC", [128, NLAT], BF16, st)
                self.ropeS = self.sb("ropeS", [128, NLAT], BF16, st)
                self.dma("pool", self.ropeC.t[:], self.ropeC_d[:, :], [], [self.ropeC], self.ropeC)
                self.dma("pool", self.ropeS.t[:], self.ropeS_d[:, :], [], [self.ropeS], self.ropeS)
            if kind == "b":
                Eb = [self.sb(f"Eb{i}", [128, self.na_nt * 128], BF16, st) for i in range(2)]
            if kind == "a":
                odf = [self.sb(f"odf{i}", [128, 128], F32, st) for i in range(3)]
                o0f = self.sb("o0f", [128, 4, 128], F32, st)
                sg = self.sb("sg", [128, 128], F32, st)
                lam = self.sb("lam", [128, 8], F32, st)
                junk = self.sb("junk", [128, 128], BF16, st)
                lo, _ = self.lay["a_lam"]
                lv = self.pp.t[:, lo:lo + 256]
                t = self.nxt("tmpf")
                self.tt("dve", t.t[:, 0:64], lv[:, 0:64], lv[:, 64:128], ALU.mult, [self.pp], [t])
                self.tt("dve", t.t[:, 64:128], lv[:, 128:192], lv[:, 192:256], ALU.mult, [self.pp], [t])
                self.op("dve", lambda e, o=lam.t[:, 0:2], i=t.t[:, 0:128].rearrange("p (a b) -> p a b", a=2):
                        e.reduce_sum(out=o, in_=i, axis=AX.X), [t], [lam])
                self.act(lam.t[:, 2:4], lam.t[:, 0:2], AF.Exp, [lam], [lam])
                lam_init = 0.8 - 0.6 * math.exp(-0.3 * li)
                self.tt("dve", lam.t[:, 4:5], lam.t[:, 3:4], lam.t[:, 2:3], ALU.subtract, [lam], [lam])
                self.ts("dve", lam.t[:, 4:5], lam.t[:, 4:5], -lam_init, None, ALU.add, None, [lam], [lam])
                self.ts("dve", sg.t[:], self.ppv("a_subln"), 1.0 - lam_init, None, ALU.mult, None, [self.pp], [sg])
            self.op("pool", lambda e, a=VA.t[:, :, :, dv:dv1]: e.memset(a, 1.0), [], [VA])

            def kreq():
                if kind != "d":
                    return None
                src = self.w[wq][:, 1024:1280].rearrange("(kc p) n -> p kc n", p=128)

                def fn(t, src=src):
                    out = []
                    for hk in range(4):
                        for dup in range(2):
                            out.append((t[:, :, hk * 128 + dup * 64: hk * 128 + dup * 64 + 64], src[:, :, hk * 64:(hk + 1) * 64]))
                    return out
                return self.wget((wq, "kdup"), fn)

            for blk in range(5):
                n = self.blk_n(blk)
                t0 = self.blk_t0(blk)
                self.load_x(s, blk)
                self.norm_mod(s, blk, li, 0, 1, self.hb, lambda c, n=n: self.hb.t[:, c, :n])
                for c in range(nkc):
                    if kind == "d":
                        wt = kreq()
                        wcol = c * 128
                    else:
                        wt = self.wtile(wq, None, kcol0 + (c // 4) * 512)
                        wcol = (c % 4) * 128
                    bank = self.ps()
                    for kc in range(8):
                        self.mm(bank.t[:, :n], bank, wt.t[:, kc, wcol:wcol + 128], self.hb.t[:, kc, :n], [wt, self.hb], kc == 0, kc == 7)
                    self.qk_post(bank, n, pre + "_kg", pre + "_kgs", t0 - NCTX, KT.t[:, c, t0:t0 + n], KT, rope and blk > 0)
                for tt_ in range(n // 128):
                    ti = t0 // 128 + tt_
                    for vb in range((nvcols + 511) // 512):
                        ncol = min(512, nvcols - vb * 512)
                        wt = self.wtile(wq, None, vcol0 + vb * 512, ncols=ncol)
                        bank = self.ps()
                        for kc in range(8):
                            self.mm(bank.t[:, :ncol], bank, self.hb.t[:, kc, tt_ * 128:(tt_ + 1) * 128], wt.t[:, kc, :ncol],
                                    [wt, self.hb], kc == 0, kc == 7)
                        nh = ncol // dv
                        h0 = vb * 512 // dv
                        self.act(VA.t[:, ti, h0:h0 + nh, 0:dv], bank.t[:, :ncol].rearrange("p (h d) -> p h d", d=dv), AF.Copy, [bank], [VA])

            for blk in range(0 if need_ctx else 1, 5):
                n = self.blk_n(blk)
                nqt = n // 128
                t0 = self.blk_t0(blk)
                self.load_x(s, blk)
                self.norm_mod(s, blk, li, 0, 1, self.hb, lambda c, n=n: self.hb.t[:, c, :n])
                for c in range(8):
                    wt = self.wtile(wq, None, (c // 4) * 512)
                    bank = self.ps()
                    for kc in range(8):
                        self.mm(bank.t[:, :n], bank, wt.t[:, kc, (c % 4) * 128:(c % 4 + 1) * 128], self.hb.t[:, kc, :n], [wt, self.hb], kc == 0, kc == 7)
                    self.qk_post(bank, n, pre + "_qg", pre + "_qgs", t0 - NCTX, QT.t[:, c, :n], QT, rope and blk > 0)
                kts = list(range(2)) if blk == 0 else list(range(NKT))
                if kind in ("a", "d"):
                    nhm = 16
                    for hm in range(nhm):
                        cq = hm // 2
                        base = (hm % 2) * 64
                        if kind == "d":
                            ck = hm // 4
                            hv = hm // 4
                        else:
                            ck = cq
                            hv = hm // 2
                        accs = [self.acc() for _ in range(nqt)]
                        for ki, kt in enumerate(kts):
                            bs = self.ps()
                            self.mm(bs.t[:, :n], bs, KT.t[base:base + 64, ck, kt * 128:(kt + 1) * 128], QT.t[base:base + 64, cq, :n],
                                    [KT, QT], True, True)
                            pt = PT[ptr[0] % 3]
                            ptr[0] += 1
                            self.act(pt.t[:, :n], bs.t[:, :n], AF.Exp, [bs], [pt], scale=SCALE)
                            for qt in range(nqt):
                                self.mm(accs[qt].t[:, 0:dv1], accs[qt], pt.t[:, qt * 128:(qt + 1) * 128], VA.t[:, kt, hv, :],
                                        [pt, VA], ki == 0, ki == len(kts) - 1)
                        sm = self.nxt("small")
                        if kind == "d":
                            for qt in range(nqt):
                                self.op("dve", lambda e, o=sm.t[:, qt:qt + 1], i=accs[qt].t[:, dv:dv1]: e.reciprocal(out=o, in_=i), [accs[qt]], [sm])
                            for qt in range(nqt):
                                self.act(otm.t[:, qt, hm * 64:(hm + 1) * 64], accs[qt].t[:, 0:dv], AF.Copy, [accs[qt], sm], [otm], scale=sm.t[:, qt:qt + 1])
                        else:
                            m = hm % 2
                            h = hm // 2
                            for qt in range(nqt):
                                self.op("dve", lambda e, o=sm.t[:, qt:qt + 1], i=accs[qt].t[:, dv:dv1]: e.reciprocal(out=o, in_=i), [accs[qt]], [sm])
                            if m == 0:
                                for qt in range(nqt):
                                    self.act(o0f.t[:, qt, :], accs[qt].t[:, 0:dv], AF.Copy, [accs[qt], sm], [o0f], scale=sm.t[:, qt:qt + 1])
                            else:
                                self.ts("dve", sm.t[:, 4:4 + nqt], sm.t[:, 0:nqt], lam.t[:, 4:5], None, ALU.mult, None, [sm, lam], [sm])
                                for qt in range(nqt):
                                    od = odf[qt % 3]
                                    self.stt(od.t[:], accs[qt].t[:, 0:dv], sm.t[:, 4 + qt:5 + qt], o0f.t[:, qt, :], ALU.mult, ALU.add,
                                             [accs[qt], sm, o0f], [od])
                                    self.act(junk.t[:], od.t[:], AF.Square, [od], [junk, sm], accum_out=sm.t[:, 8 + qt:9 + qt])
                                    self.act(sm.t[:, 12 + qt:13 + qt], sm.t[:, 8 + qt:9 + qt], AF.Sqrt, [sm, self.eps], [sm],
                                             bias=self.eps.t[:, 0:1], scale=1.0 / 128)
                                    self.op("dve", lambda e, o=sm.t[:, 12 + qt:13 + qt]: e.reciprocal(out=o, in_=o), [sm], [sm])
                                    self.stt(otm.t[:, qt, h * 128:(h + 1) * 128], od.t[:], sm.t[:, 12 + qt:13 + qt], sg.t[:], ALU.mult, ALU.mult,
                                             [od, sm, sg], [otm])
                else:
                    for h in range(16):
                        cq = h // 2
                        base = (h % 2) * 64
                        if blk > 0:
                            E = Eb[h % 2]
                            self.dma("sp", E.t[:], self.naE[h], [self.DnaE], [E], E)
                        for qt in range(nqt):
                            if blk == 0:
                                tiles = [(0, None), (1, None)]
                            else:
                                g = (blk - 1) * 4 + qt
                                tiles = [(0, None), (1, None)] + [(2 + kl, ty) for (kl, ty) in self.na_plan[g]]
                            pt = PT[ptr[0] % 3]
                            ptr[0] += 1
                            nt_ = len(tiles)
                            banks = [self.ps() for _ in range((nt_ + 3) // 4)]
                            for j, (kt, ty) in enumerate(tiles):
                                bk = banks[j // 4]
                                self.mm(bk.t[:, (j % 4) * 128:(j % 4 + 1) * 128], bk, KT.t[base:base + 64, cq, kt * 128:(kt + 1) * 128],
                                        QT.t[base:base + 64, cq, qt * 128:(qt + 1) * 128], [KT, QT], True, True)
                            for bi, bk in enumerate(banks):
                                w_ = min(4, nt_ - bi * 4) * 128
                                self.act(pt.t[:, bi * 512: bi * 512 + w_], bk.t[:, :w_], AF.Exp, [bk], [pt], scale=SCALE)
                            if blk > 0:
                                tys = [ty for (_, ty) in tiles[2:]]
                                if tys == [0, 1, 2, 3, 4]:
                                    self.tt("dve", pt.t[:, 256:896], pt.t[:, 256:896], E.t[:, 0:640], ALU.mult, [pt, E], [pt])
                                else:
                                    for j, ty in enumerate(tys):
                                        self.tt("dve", pt.t[:, 256 + j * 128: 384 + j * 128], pt.t[:, 256 + j * 128: 384 + j * 128],
                                                E.t[:, ty * 128:(ty + 1) * 128], ALU.mult, [pt, E], [pt])
                            ac = self.acc()
                            for j, (kt, ty) in enumerate(tiles):
                                self.mm(ac.t[:, 0:dv1], ac, pt.t[:, j * 128:(j + 1) * 128], VA.t[:, kt, h, :], [pt, VA], j == 0, j == nt_ - 1)
                            sm = self.nxt("small")
                            self.op("dve", lambda e, o=sm.t[:, 0:1], i=ac.t[:, dv:dv1]: e.reciprocal(out=o, in_=i), [ac], [sm])
                            self.act(otm.t[:, qt, h * 64:(h + 1) * 64], ac.t[:, 0:dv], AF.Copy, [ac, sm], [otm], scale=sm.t[:, 0:1])
                for c in range(8):
                    bank = self.ps()
                    bv = bank.t[:].bitcast(BF16)
                    for qt in range(nqt):
                        self.op("pe", lambda e, o=bv[:, qt * 128:(qt + 1) * 128], i=otm.t[:, qt, c * 128:(c + 1) * 128], idn=self.ident:
                                e.transpose(o, i, idn), [otm, self.cst], [bank])
                    self.op("dve", lambda e, o=QT.t[:, c, :n], i=bv[:, :n]: e.tensor_copy(out=o, in_=i), [bank], [QT])
                for co in range(8):
                    wt = self.wtile(wo, None, (co // 4) * 512)
                    bank = self.ps()
                    for c in range(8):
                        self.mm(bank.t[:, :n], bank, wt.t[:, c, (co % 4) * 128:(co % 4 + 1) * 128], QT.t[:, c, :n], [wt, QT], c == 0, c == 7)
                    self.update_x(s, blk, co, 0, n, bank, li, 2)
                if blk == 0:
                    self.c_in_s[s] = True
            self.x_in_out[s] = True

    def preamble_na(self):
        with self.scope() as st:
            nt = self.na_nt
            bt = [self.sb(f"nab{i}", [128, nt * 128], F32, st) for i in range(2)]
            mk = self.sb("nam", [128, nt * 128], F32, st)
            eo = [self.sb(f"nae{i}", [128, nt * 128], BF16, st) for i in range(2)]
            self.dma("sp", mk.t[:], self.nam_d.rearrange("p t q -> p (t q)"), [], [mk], mk)
            for h in range(16):
                b = bt[h % 2]
                o = eo[h % 2]
                self.dma("sp", b.t[:], self.nab_d[h].rearrange("p t q -> p (t q)"), [], [b], b)
                self.act(b.t[:], b.t[:], AF.Exp, [b], [b])
                self.tt("dve", o.t[:], b.t[:], mk.t[:], ALU.mult, [b, mk], [o])
                self.dma("sp", self.naE[h], o.t[:], [o], [self.DnaE], o)

    def lru_layer(self, li, s):
        need_ctx = li < 3
        self.Dx2 = self.Dx
        segs = [(0, NCTX), (NCTX, T)]
        ranges = [(0, 256)] + [(256 + 512 * j, 512) for j in range(4)]
        with self.scope() as st:
            hT = self.sb("hT", [128, 8, T], BF16, st)
            zst = self.sb("zst", [128, T], BF16, st)
            U0x = Tile(self.xb.t[:].rearrange("p c n -> p (c n)"), self.xb.b)
            U1 = [self.sb(f"U1{i}", [128, T], F32, st) for i in range(2)]
            Ub = [self.sb(f"Ub{i}", [128, T], BF16, st) for i in range(2)]
            Rr = self.sb("Rr", [128, T], F32, st)
            Ii = self.sb("Ii", [128, T], F32, st)
            HF = self.sb("HF", [128, T], F32, st)
            HB = self.sb("HB", [128, T], F32, st)
            Mm = HB
            nsp = self.sb("nsp", [128, 2, 8], F32, st)
            spt = [self.sb(f"spt{i}", [128, 16], F32, st) for i in range(4)]
            for d_, nm in enumerate(("c_fwd_lam", "c_bwd_lam")):
                self.act(spt[0].t[:, d_ * 8:(d_ + 1) * 8], self.ppv(nm), AF.Exp, [self.pp], [spt[0]], scale=-1.0)
            xx = spt[0].t[:, 0:16]
            self.ts("dve", spt[1].t[:], xx, -0.25, 1.0 / 3.0, ALU.mult, ALU.add, [spt[0]], [spt[1]])
            self.tt("dve", spt[1].t[:], spt[1].t[:], xx, ALU.mult, [spt[1], spt[0]], [spt[1]])
            self.ts("dve", spt[1].t[:], spt[1].t[:], -1.0, 0.5, ALU.mult, ALU.add, [spt[1]], [spt[1]])
            self.tt("dve", spt[1].t[:], spt[1].t[:], xx, ALU.mult, [spt[1], spt[0]], [spt[1]])
            self.ts("dve", spt[1].t[:], spt[1].t[:], -1.0, 1.0, ALU.mult, ALU.add, [spt[1]], [spt[1]])
            self.tt("dve", spt[1].t[:], spt[1].t[:], xx, ALU.mult, [spt[1], spt[0]], [spt[1]])
            self.act(spt[2].t[:], xx, AF.Ln, [spt[0]], [spt[2]], bias=1.0, scale=1.0)
            self.ts("dve", spt[3].t[:], xx, 0.05, None, ALU.is_lt, None, [spt[0]], [spt[3]])
            self.tt("dve", spt[1].t[:], spt[1].t[:], spt[2].t[:], ALU.subtract, [spt[1], spt[2]], [spt[1]])
            self.tt("dve", spt[1].t[:], spt[1].t[:], spt[3].t[:], ALU.mult, [spt[1], spt[3]], [spt[1]])
            self.tt("dve", spt[1].t[:], spt[1].t[:], spt[2].t[:], ALU.add, [spt[1], spt[2]], [spt[1]])
            self.ts("dve", nsp.t[:].rearrange("p a b -> p (a b)"), spt[1].t[:], -8.0, None, ALU.mult, None, [spt[1]], [nsp])

            for blk in range(5):
                n = self.blk_n(blk)
                t0 = self.blk_t0(blk)
                self.load_x(s, blk)
                self.norm_mod(s, blk, li, 0, 1, hT, lambda c, n=n, t0=t0: hT.t[:, c, t0:t0 + n])
            cwo, _ = self.lay["c_convw"]
            cbo, _ = self.lay["c_convb"]
            for nb in range(4):
                for cc in range(2):
                    c = nb * 2 + cc
                    for (r0, rn) in ranges:
                        wt = self.wtile("c_w_in", None, 1024 + (c // 4) * 512)
                        bank = self.ps()
                        for kc in range(8):
                            self.mm(bank.t[:, :rn], bank, wt.t[:, kc, (c % 4) * 128:(c % 4 + 1) * 128], hT.t[:, kc, r0:r0 + rn], [wt, hT], kc == 0, kc == 7)
                        self.act(U0x.t[:, r0:r0 + rn], bank.t[:, :rn], AF.Copy, [bank], [U0x])
                    wv = lambda j, c=c: self.pp.t[:, cwo + j * 8 + c: cwo + j * 8 + c + 1]
                    cb = self.pp.t[:, cbo + c: cbo + c + 1]
                    for (a, b) in segs:
                        self.ts("dve", U1[cc].t[:, a:b], U0x.t[:, a:b], wv(2), cb, ALU.mult, ALU.add, [U0x, self.pp], [U1[cc]])
                        self.stt(U1[cc].t[:, a + 2:b], U0x.t[:, a:b - 2], wv(0), U1[cc].t[:, a + 2:b], ALU.mult, ALU.add, [U0x, self.pp, U1[cc]], [U1[cc]])
                        self.stt(U1[cc].t[:, a + 1:b], U0x.t[:, a:b - 1], wv(1), U1[cc].t[:, a + 1:b], ALU.mult, ALU.add, [U0x, self.pp, U1[cc]], [U1[cc]])
                        self.stt(U1[cc].t[:, a:b - 1], U0x.t[:, a + 1:b], wv(3), U1[cc].t[:, a:b - 1], ALU.mult, ALU.add, [U0x, self.pp, U1[cc]], [U1[cc]])
                    self.act(Ub[cc].t[:], U1[cc].t[:], AF.Copy, [U1[cc]], [Ub[cc]])
                for cc in range(2):
                    c = nb * 2 + cc
                    for d_ in range(2):
                        def fn(t, d_=d_, nb=nb):
                            wa = self.w["c_gw"][d_ * 2 + 0, nb].rearrange("(kc p) n -> p kc n", p=128)
                            wx = self.w["c_gw"][d_ * 2 + 1, nb].rearrange("(kc p) n -> p kc n", p=128)
                            return [(t[:, 0:2, 0:256], wa), (t[:, 2:4, 0:256], wx)]
                        wt = self.wget(("c_gw", d_, nb), fn)
                        ba = self.ppv("c_fwd_b_a" if d_ == 0 else "c_bwd_b_a")[:, c:c + 1]
                        bx = self.ppv("c_fwd_b_x" if d_ == 0 else "c_bwd_b_x")[:, c:c + 1]
                        for (r0, rn) in ranges:
                            for gi, (dst, bb_) in enumerate(((Rr, ba), (Ii, bx))):
                                bank = self.ps()
                                for k2 in range(2):
                                    self.mm(bank.t[:, :rn], bank, wt.t[:, gi * 2 + k2, cc * 128:(cc + 1) * 128], Ub[k2].t[:, r0:r0 + rn],
                                            [wt, Ub[k2]], k2 == 0, k2 == 1)
                                self.act(dst.t[:, r0:r0 + rn], bank.t[:, :rn], AF.Sigmoid, [bank, self.pp], [dst], bias=bb_, scale=1.0)
                        self.act(Rr.t[:], Rr.t[:], AF.Exp, [Rr, nsp], [Rr], scale=nsp.t[:, d_, c:c + 1])
                        self.tt("pool", Mm.t[:], Rr.t[:], Rr.t[:], ALU.mult, [Rr], [Mm])
                        self.ts("dve", Mm.t[:], Mm.t[:], -1.0, 1.0, ALU.mult, ALU.add, [Mm], [Mm])
                        self.ts("dve", Mm.t[:], Mm.t[:], 0.0, None, ALU.max, None, [Mm], [Mm])
                        self.act(Mm.t[:], Mm.t[:], AF.Sqrt, [Mm], [Mm])
                        self.tt("pool", Ii.t[:], Ii.t[:], U1[cc].t[:], ALU.mult, [Ii, U1[cc]], [Ii])
                        self.tt("dve", Ii.t[:], Ii.t[:], Mm.t[:], ALU.mult, [Ii, Mm], [Ii])
                        if d_ == 0:
                            self.op("dve", lambda e, o=HF.t[:], a=Rr.t[:], b=Ii.t[:]:
                                    e.tensor_tensor_scan(out=o, data0=a, data1=b, initial=0.0, op0=ALU.mult, op1=ALU.add), [Rr, Ii], [HF])
                        else:
                            self.op("dve", lambda e, o=HB.t[:, 0:NCTX][:, ::-1], a=Rr.t[:, 0:NCTX][:, ::-1], b=Ii.t[:, 0:NCTX][:, ::-1]:
                                    e.tensor_tensor_scan(out=o, data0=a, data1=b, initial=0.0, op0=ALU.mult, op1=ALU.add), [Rr, Ii], [HB])
                            self.op("dve", lambda e, o=HB.t[:, NCTX:T][:, ::-1], a=Rr.t[:, NCTX:T][:, ::-1], b=Ii.t[:, NCTX:T][:, ::-1], ini=HB.t[:, 0:1]:
                                    e.tensor_tensor_scan(out=o, data0=a, data1=b, initial=ini, op0=ALU.mult, op1=ALU.add), [Rr, Ii, HB], [HB])
                    self.tt("pool", HF.t[:], HF.t[:], HB.t[:], ALU.add, [HF, HB], [HF])
                    for (r0, rn) in ranges:
                        wt = self.wtile("c_w_in", None, (c // 4) * 512)
                        bank = self.ps()
                        for kc in range(8):
                            self.mm(bank.t[:, :rn], bank, wt.t[:, kc, (c % 4) * 128:(c % 4 + 1) * 128], hT.t[:, kc, r0:r0 + rn], [wt, hT], kc == 0, kc == 7)
                        self.act(HB.t[:, r0:r0 + rn], bank.t[:, :rn], AF.Gelu_apprx_tanh, [bank], [HB])
                    self.tt("dve", zst.t[:], HF.t[:], HB.t[:], ALU.mult, [HF, HB], [zst])
                    self.dma("sp", self.zS[s, c * 128:(c + 1) * 128, :], zst.t[:], [zst], [self.DzS[s]], zst)
            for blk in range(0 if need_ctx else 1, 5):
                n = self.blk_n(blk)
                t0 = self.blk_t0(blk)
                zsrc = self.zS[s, :, t0:t0 + n].rearrange("(c p) t -> p c t", p=128)
                self.dma("sp", self.hb.t[:, :, :n], zsrc, [self.DzS[s]], [self.hb], self.hb)
                for co in range(8):
                    wt = self.wtile("c_w_o", None, (co // 4) * 512)
                    bank = self.ps()
                    for c in range(8):
                        self.mm(bank.t[:, :n], bank, wt.t[:, c, (co % 4) * 128:(co % 4 + 1) * 128], self.hb.t[:, c, :n], [wt, self.hb], c == 0, c == 7)
                    self.update_x(s, blk, co, 0, n, bank, li, 2)
                if blk == 0:
                    self.c_in_s[s] = True
            self.x_in_out[s] = True


def run_cores(inputs, nseq=4, cores=8, layers=(0, 1, 2, 3)):
    inp = {k: np.asarray(v) for k, v in inputs.items()}
    pp, lay = build_pp(inp)
    plan, nt, nab, nam = na_tables(inp["b_rpb"][0])
    rC, rS = rope_tables()
    cst = consts_arr()
    shared = {
        "pp": pp, "cst": cst, "ropeC": rC, "ropeS": rS, "nabias": nab, "namask": nam,
        "w_mod": np.ascontiguousarray(inp["w_mod"], np.float32),
        "w_mlp1": np.ascontiguousarray(inp["w_mlp1"], np.float32),
        "w_mlp2": np.ascontiguousarray(inp["w_mlp2"], np.float32),
        "a_w_qkv": np.ascontiguousarray(inp["a_w_qkv"][0]), "a_w_o": np.ascontiguousarray(inp["a_w_o"][0]),
        "b_w_qkv": np.ascontiguousarray(inp["b_w_qkv"][0]), "b_w_o": np.ascontiguousarray(inp["b_w_o"][0]),
        "c_w_in": np.ascontiguousarray(inp["c_w_in"][0]), "c_w_o": np.ascontiguousarray(inp["c_w_o"][0]),
        "c_gw": np.ascontiguousarray(np.stack([inp["c_fwd_w_a"][0], inp["c_fwd_w_x"][0], inp["c_bwd_w_a"][0], inp["c_bwd_w_x"][0]], axis=0)),
        "d_w_qkv": np.ascontiguousarray(inp["d_w_qkv"][0]), "d_w_o": np.ascontiguousarray(inp["d_w_o"][0]),
    }
    b = Builder(nseq, tuple(layers), lay, nt, plan)
    nc = b.build()
    in_maps = []
    for ci in range(cores):
        b0 = ci * nseq
        m = dict(shared)
        m["xT"] = np.ascontiguousarray(inp["x"][b0:b0 + nseq].transpose(0, 2, 1))
        m["cxT"] = np.ascontiguousarray(inp["ctx"][b0:b0 + nseq].transpose(0, 2, 1))
        sc = np.zeros((128, 8, 8), np.float32)
        sc[:, :, :nseq] = inp["c"][b0:b0 + nseq].reshape(nseq, 8, 128).transpose(2, 1, 0)
        sc[:, :, 4] = inp["c_ctx"].reshape(8, 128).T
        m["scT"] = sc
        in_maps.append(m)
    res = run_bass_kernel_spmd(nc, in_maps, core_ids=list(range(cores)))
    outs = [np.asarray(r["outT"]).transpose(0, 2, 1) for r in res.results]
    return np.ascontiguousarray(np.concatenate(outs, axis=0).astype(np.float32))


def kernel(**inputs):
    return run_cores(inputs, nseq=4, cores=8, layers=(0, 1, 2, 3))
```

```python
import math
import contextlib
import numpy as np
import concourse.bass as bass
import concourse.mybir as mybir
from concourse.bass_utils import run_bass_kernel_spmd

F32 = mybir.dt.float32
BF16 = mybir.dt.bfloat16
AF = mybir.ActivationFunctionType
ALU = mybir.AluOpType
AX = mybir.AxisListType

D = 1024
NCTX = 256
NLAT = 2048
T = NCTX + NLAT
NKT = T // 128
EPS = 1e-6
SCALE = 0.125
ENGINES = ("pe", "act", "dve", "pool", "sp")


class Buf:
    __slots__ = ("name", "last_w", "readers", "dsem_id")

    def __init__(self, name, last_w=None):
        self.name = name
        self.last_w = last_w
        self.readers = []
        self.dsem_id = None


class Op:
    __slots__ = ("eng", "fn", "reads", "writes", "idx", "eidx", "deps", "signal",
                 "tick", "is_dma", "dsem", "waits")

    def __init__(self, eng, fn, reads, writes, is_dma=False, dsem=None):
        self.eng = eng
        self.fn = fn
        self.reads = reads
        self.writes = writes
        self.is_dma = is_dma
        self.dsem = dsem
        self.signal = False
        self.tick = None
        self.deps = []
        self.waits = []


class Prog:
    def __init__(self, nc):
        self.nc = nc
        self.ops = []
        self.deferred = []
        self.n_dsem = 0

    def op(self, eng, fn, reads=(), writes=()):
        o = Op(eng, fn, tuple(reads), tuple(writes))
        self.ops.append(o)
        return o

    def _mkdma(self, eng, fn, reads, writes, sem_buf):
        if sem_buf.dsem_id is None:
            sem_buf.dsem_id = self.n_dsem
            self.n_dsem += 1
        return Op(eng, fn, tuple(reads), tuple(writes), is_dma=True, dsem=sem_buf.dsem_id)

    def dma(self, eng, fn, reads, writes, sem_buf):
        o = self._mkdma(eng, fn, reads, writes, sem_buf)
        self.ops.append(o)
        return o

    def dma_at(self, pos, eng, fn, reads, writes, sem_buf):
        o = self._mkdma(eng, fn, reads, writes, sem_buf)
        self.deferred.append((pos, len(self.deferred), o))
        return o

    def pos(self):
        return len(self.ops)

    def finalize(self):
        if self.deferred:
            self.deferred.sort(key=lambda x: (x[0], x[1]))
            out = []
            di = 0
            nd = len(self.deferred)
            for i, o in enumerate(self.ops):
                while di < nd and self.deferred[di][0] <= i:
                    out.append(self.deferred[di][2])
                    di += 1
                out.append(o)
            while di < nd:
                out.append(self.deferred[di][2])
                di += 1
            self.ops = out
            self.deferred = []
        self.eng_ops = {e: [] for e in ENGINES}
        for i, o in enumerate(self.ops):
            o.idx = i
            o.eidx = len(self.eng_ops[o.eng])
            self.eng_ops[o.eng].append(o)

    def resolve(self):
        self.finalize()
        for o in self.ops:
            deps = {}
            for b in o.reads:
                if b.last_w is not None:
                    deps[b.last_w.idx] = b.last_w
            for b in o.writes:
                if b.last_w is not None:
                    deps[b.last_w.idx] = b.last_w
                for r in b.readers:
                    deps[r.idx] = r
            for b in o.writes:
                b.last_w = o
                b.readers = []
            for b in o.reads:
                if b.last_w is not o:
                    b.readers.append(o)
            deps.pop(o.idx, None)
            o.deps = list(deps.values())
        dma_count = {}
        dma_pos = {}
        for o in self.ops:
            if o.is_dma:
                dma_count[o.dsem] = dma_count.get(o.dsem, 0) + 1
                dma_pos[o.idx] = dma_count[o.dsem]
        waited = {e: {} for e in ENGINES}
        for o in self.ops:
            need = {}
            for d in o.deps:
                if d.is_dma:
                    key = ("d", d.dsem)
                    pos = dma_pos[d.idx]
                else:
                    if d.eng == o.eng and not o.is_dma:
                        if o.eng == "pe":
                            continue
                        if o.eidx - d.eidx > 2:
                            continue
                    key = ("e", d.eng)
                    pos = d.eidx
                if key not in need or need[key][0] < pos:
                    need[key] = (pos, d)
            w = waited[o.eng]
            for key, (pos, d) in need.items():
                if w.get(key, -1) >= pos:
                    continue
                w[key] = pos
                d.signal = True
                o.waits.append(d)
        ecount = {e: 0 for e in ENGINES}
        dcount = {}
        for o in self.ops:
            if o.is_dma:
                dcount[o.dsem] = dcount.get(o.dsem, 0) + 16
                o.tick = dcount[o.dsem]
            elif o.signal:
                ecount[o.eng] += 1
                o.tick = ecount[o.eng]

    def emit(self, final_waits=()):
        nc = self.nc
        self.resolve()
        with contextlib.ExitStack() as st:
            esem = {e: st.enter_context(nc.semaphore(f"s_{e}")) for e in ENGINES}
            dsem = [st.enter_context(nc.semaphore(f"d_{i}")) for i in range(self.n_dsem)]
            block = st.enter_context(nc.Block())

            def run(eng_name, eng):
                for o in self.eng_ops[eng_name]:
                    for d in o.waits:
                        if d.is_dma:
                            eng.wait_ge(dsem[d.dsem], d.tick)
                        else:
                            eng.wait_ge(esem[d.eng], d.tick)
                    ins = o.fn(eng)
                    if o.is_dma:
                        ins.then_inc(dsem[o.dsem], 16)
                    elif o.signal:
                        ins.then_inc(esem[o.eng], 1)
                if eng_name == "sp":
                    seen = {}
                    for d in final_waits:
                        seen[d.dsem] = max(seen.get(d.dsem, 0), d.tick)
                    for k, v in seen.items():
                        eng.wait_ge(dsem[k], v)

            @block.tensor
            def _(e):
                run("pe", e)

            @block.scalar
            def _(e):
                run("act", e)

            @block.vector
            def _(e):
                run("dve", e)

            @block.gpsimd
            def _(e):
                run("pool", e)

            @block.sync
            def _(e):
                run("sp", e)


class Tile:
    __slots__ = ("t", "b")

    def __init__(self, t, b):
        self.t = t
        self.b = b


def _pvec(v):
    return np.ascontiguousarray(np.asarray(v, np.float32).reshape(8, 128).T)


def _gvec(g, swap=False):
    idx = np.arange(128) % 64
    if swap:
        idx = idx ^ 1
    return np.asarray(g, np.float32)[idx].reshape(128, 1)


def _bc(v):
    v = np.asarray(v, np.float32).reshape(1, -1)
    return np.repeat(v, 128, axis=0)


def rope_tables():
    t = np.arange(NLAT)
    row = (t // 64).astype(np.float32)
    col = (t % 64).astype(np.float32)
    inv = (np.float32(10000.0) ** (-np.arange(16, dtype=np.float32) / np.float32(16))).astype(np.float32)
    ang = np.concatenate([row[:, None] * inv, col[:, None] * inv], axis=-1).astype(np.float32)
    cos = np.cos(ang).astype(np.float32)
    sin = np.sin(ang).astype(np.float32)
    p = np.arange(128)
    i = (p % 64) // 2
    sign = np.where(p % 2 == 0, -1.0, 1.0).astype(np.float32)
    C = np.ascontiguousarray(cos[:, i].T)
    S = np.ascontiguousarray(sin[:, i].T * sign[:, None])
    return C.astype(np.float32), S.astype(np.float32)


def _r0(r):
    return min(max(r - 4, 0), 24)


def na_plan():
    types = {}
    order = []
    plan = []

    def get_type(dt_, pat):
        key = (dt_, pat)
        if key not in types:
            types[key] = len(order)
            order.append(key)
        return types[key]

    r = 8
    for kb in range(r - 4, r + 6, 2):
        pat = tuple(tuple(1 if (_r0(r + qp) <= kb + kp < _r0(r + qp) + 8) else 0 for qp in (0, 1)) for kp in (0, 1))
        get_type(kb - r, pat)
    for g in range(16):
        r = 2 * g
        lst = []
        for kb in range(0, 32, 2):
            pat = tuple(tuple(1 if (_r0(r + qp) <= kb + kp < _r0(r + qp) + 8) else 0 for qp in (0, 1)) for kp in (0, 1))
            if not any(any(x) for x in pat):
                continue
            lst.append((kb // 2, get_type(kb - r, pat)))
        plan.append(lst)
    return plan, order


def na_tables(rpb):
    plan, order = na_plan()
    nt = len(order)
    rpb = np.asarray(rpb, np.float32)
    kp = np.arange(128) // 64
    kc = np.arange(128) % 64
    qp = np.arange(128) // 64
    qc = np.arange(128) % 64
    c0 = np.clip(qc - 8, 0, 48)
    colin = (kc[:, None] >= c0[None, :]) & (kc[:, None] < c0[None, :] + 16)
    cidx = np.clip(kc[:, None] - qc[None, :], -15, 15) + 15
    bias = np.zeros((16, 128, nt, 128), np.float32)
    mask = np.zeros((128, nt, 128), np.float32)
    for ti, (dt_, pat) in enumerate(order):
        patm = np.asarray(pat, np.float32)
        valid = patm[kp[:, None], qp[None, :]] * colin
        ridx = np.clip(dt_ + kp[:, None] - qp[None, :] + 7, 0, 14)
        bias[:, :, ti, :] = rpb[:, ridx, cidx]
        mask[:, ti, :] = valid
    return plan, nt, bias, mask


PP_LAYOUT = None


def build_pp(inp):
    items = []

    def add(name, arr):
        arr = np.asarray(arr, np.float32).reshape(128, -1)
        items.append((name, arr))

    add("n1g", np.stack([_pvec(inp["norm1_g"][i]) for i in range(4)], axis=1))
    add("n2g", np.stack([_pvec(inp["norm2_g"][i]) for i in range(4)], axis=1))
    add("bmod", np.stack([inp["b_mod"][i].reshape(6, 8, 128).transpose(2, 0, 1).reshape(128, 48) for i in range(4)], axis=1))
    for pre, qn, kn in (("a", "a_q_norm_g", "a_k_norm_g"), ("b", "b_q_norm_g", "b_k_norm_g"), ("d", "d_q_norm_g", "d_k_norm_g")):
        add(pre + "_qg", _gvec(inp[qn][0]))
        add(pre + "_qgs", _gvec(inp[qn][0], True))
        add(pre + "_kg", _gvec(inp[kn][0]))
        add(pre + "_kgs", _gvec(inp[kn][0], True))
    add("a_lam", np.concatenate([_bc(inp[n][0]) for n in ("a_lambda_q1", "a_lambda_k1", "a_lambda_q2", "a_lambda_k2")], axis=1))
    add("a_subln", _bc(inp["a_subln_g"][0]))
    add("c_convw", inp["c_conv_w"][0].reshape(4, 8, 128).transpose(2, 0, 1).reshape(128, 32))
    add("c_convb", _pvec(inp["c_conv_b"][0]))
    for n in ("c_fwd_b_a", "c_fwd_b_x", "c_fwd_lam", "c_bwd_b_a", "c_bwd_b_x", "c_bwd_lam"):
        add(n, _pvec(inp[n][0]))
    lay = {}
    off = 0
    for n, a in items:
        lay[n] = (off, a.shape[1])
        off += a.shape[1]
    return np.ascontiguousarray(np.concatenate([a for _, a in items], axis=1)), lay


def consts_arr():
    p = np.arange(128)
    ident = np.eye(128, dtype=np.float32)
    bones = (p[:, None] // 64 == p[None, :] // 64).astype(np.float32)
    swap = (p[:, None] == (p[None, :] ^ 1)).astype(np.float32)
    ones = np.ones((128, 128), np.float32)
    return np.ascontiguousarray(np.stack([ident, bones, swap, ones], axis=1))


class Builder:
    def __init__(self, nseq, layers, pp_lay, na_nt, na_plan_):
        self.nseq = nseq
        self.layers = layers
        self.lay = pp_lay
        self.na_nt = na_nt
        self.na_plan = na_plan_
        self.nc = bass.Bass("TRN2", target_bir_lowering=False)
        self.P = Prog(self.nc)
        self.gst = contextlib.ExitStack()
        self.barrier_op = None
        self.scope_bufs = None
        self.uid = 0

    def newbuf(self, name):
        b = Buf(name, self.barrier_op)
        if self.scope_bufs is not None:
            self.scope_bufs.append(b)
        return b

    def sb(self, name, shape, dt, st=None):
        self.uid += 1
        t = (st or self.gst).enter_context(self.nc.sbuf_tensor(f"{name}_{self.uid}", list(shape), dt))
        return Tile(t, self.newbuf(name))

    @contextlib.contextmanager
    def scope(self):
        st = contextlib.ExitStack()
        old = self.scope_bufs
        self.scope_bufs = []
        self.cur_st = st
        try:
            yield st
        finally:
            bufs = self.scope_bufs
            d = self.dummy
            self.barrier_op = self.P.op("pool", lambda e, a=d.t[:, 0:1]: e.memset(a, 0.0), bufs, bufs + [d.b])
            self.scope_bufs = old
            st.close()

    def dram_in(self, name, shape, dt=F32):
        return self.nc.dram_tensor(name, list(shape), dt, kind="ExternalInput").ap()

    def op(self, eng, fn, R, W):
        return self.P.op(eng, fn, [x.b if isinstance(x, Tile) else x for x in R],
                         [x.b if isinstance(x, Tile) else x for x in W])

    def mm(self, out_ap, out_t, lhsT, rhs, R, start, stop):
        return self.op("pe", lambda e, o=out_ap, l=lhsT, r=rhs, s=start, p=stop:
                       e.matmul(o, lhsT=l, rhs=r, start=s, stop=p), R, [out_t])

    def act(self, out, in_, func, R, W, bias=None, scale=None, accum_out=None):
        kw = {}
        if bias is not None:
            kw["bias"] = bias
        if scale is not None:
            kw["scale"] = scale
        if accum_out is not None:
            kw["accum_out"] = accum_out
        return self.op("act", lambda e, o=out, i=in_, f=func, kw=kw: e.activation(out=o, in_=i, func=f, **kw), R, W)

    def tt(self, eng, out, in0, in1, op, R, W):
        return self.op(eng, lambda e, o=out, a=in0, b=in1, p=op: e.tensor_tensor(out=o, in0=a, in1=b, op=p), R, W)

    def ts(self, eng, out, in0, s1, s2, op0, op1, R, W):
        if s2 is None:
            return self.op(eng, lambda e, o=out, a=in0, x=s1, p0=op0:
                           e.tensor_scalar(out=o, in0=a, scalar1=x, scalar2=None, op0=p0), R, W)
        return self.op(eng, lambda e, o=out, a=in0, x=s1, y=s2, p0=op0, p1=op1:
                       e.tensor_scalar(out=o, in0=a, scalar1=x, scalar2=y, op0=p0, op1=p1), R, W)

    def stt(self, out, in0, scalar, in1, op0, op1, R, W):
        return self.op("dve", lambda e, o=out, a=in0, s=scalar, b=in1, p0=op0, p1=op1:
                       e.scalar_tensor_tensor(out=o, in0=a, scalar=s, in1=b, op0=p0, op1=p1), R, W)

    def dma(self, eng, out, in_, R, W, sem):
        return self.P.dma(eng, lambda e, o=out, i=in_: e.dma_start(out=o, in_=i),
                          [x.b if isinstance(x, Tile) else x for x in R],
                          [x.b if isinstance(x, Tile) else x for x in W], sem.b if isinstance(sem, Tile) else sem)

    def ps(self):
        i = self.ps_rot
        self.ps_rot = (self.ps_rot + 1) % 4
        return self.pb[i]

    def acc(self):
        i = self.acc_rot
        self.acc_rot = (self.acc_rot + 1) % 4
        return self.pb[4 + i]

    NSLOT = 6
    PD = 2

    def wget(self, key, fn):
        r = self.ring
        if key in r["valid"]:
            return self.wslots[r["valid"][key]]
        n = r["n"]
        r["n"] += 1
        j = n + self.PD - self.NSLOT
        if j >= 0 and r["keys"][j] in r["valid"] and r["valid"][r["keys"][j]] == j % self.NSLOT:
            del r["valid"][r["keys"][j]]
        r["keys"].append(key)
        r["pos"].append(self.P.pos())
        slot = n % self.NSLOT
        for k2 in [k for k, v in r["valid"].items() if v == slot]:
            del r["valid"][k2]
        ipos = r["pos"][n - self.PD] if n >= self.PD else 0
        w = self.wslots[slot]
        for (o, i) in fn(w.t):
            self.P.dma_at(ipos, "pool", lambda e, o=o, i=i: e.dma_start(out=o, in_=i), [], [w.b], w.b)
        r["valid"][key] = slot
        return w

    def build(self):
        nc = self.nc
        nseq = self.nseq
        L = self.lay
        self.xT = self.dram_in("xT", [nseq, D, NLAT])
        self.cxT = self.dram_in("cxT", [nseq, D, NCTX])
        self.scT_d = self.dram_in("scT", [128, 8, 8])
        npp = sum(w for _, w in L.values())
        self.pp_d = self.dram_in("pp", [128, npp])
        self.cst_d = self.dram_in("cst", [128, 4, 128])
        self.ropeC_d = self.dram_in("ropeC", [128, NLAT])
        self.ropeS_d = self.dram_in("ropeS", [128, NLAT])
        self.w = {}
        for n, shp in (("w_mod", [4, D, 6 * D]), ("w_mlp1", [4, D, 4 * D]), ("w_mlp2", [4, 4 * D, D]),
                       ("a_w_qkv", [D, 3 * D]), ("a_w_o", [D, D]), ("b_w_qkv", [D, 3 * D]), ("b_w_o", [D, D]),
                       ("c_w_in", [D, 2 * D]), ("c_gw", [4, 4, 256, 256]), ("c_w_o", [D, D]),
                       ("d_w_qkv", [D, 1536]), ("d_w_o", [D, D])):
            self.w[n] = self.dram_in(n, shp)
        self.nab_d = self.dram_in("nabias", [16, 128, self.na_nt, 128])
        self.nam_d = self.dram_in("namask", [128, self.na_nt, 128])
        self.outT = nc.dram_tensor("outT", [nseq, D, NLAT], F32, kind="ExternalOutput").ap()
        self.cxS = nc.dram_tensor("cxS", [nseq, D, NCTX], F32, kind="Internal").ap()
        self.naE = nc.dram_tensor("naE", [16, 128, self.na_nt * 128], BF16, kind="Internal").ap()
        self.zS = nc.dram_tensor("zS", [nseq, D, T], BF16, kind="Internal").ap()
        self.DzS = [Buf(f"dz{s}") for s in range(nseq)]
        self.Dx = [[Buf(f"dx{s}_{b}") for b in range(5)] for s in range(nseq)]
        self.DnaE = Buf("naE")
        self.x_in_out = [False] * nseq
        self.c_in_s = [False] * nseq

        self.dummy = self.sb("dummy", [128, 8], F32)
        self.pp = self.sb("pp", [128, npp], F32)
        self.cst = self.sb("cst", [128, 4, 128], BF16)
        self.eps = self.sb("eps", [128, 1], F32)
        self.scS = self.sb("scS", [128, 8, 8], F32)
        self.modT = self.sb("modT", [128, 4, 48, 8], F32)
        self.xb = self.sb("xb", [128, 8, 512], F32)
        self.hb = self.sb("hb", [128, 8, 512], BF16)
        self.sqc = [self.sb(f"sqc{i}", [128, 512], BF16) for i in range(4)]
        self.std = self.sb("std", [128, 512], F32)
        self.rstd = self.sb("rstd", [128, 512], F32)
        self.tmpf = [self.sb(f"tmpf{i}", [128, 512], F32) for i in range(3)]
        self.xu = [self.sb(f"xu{i}", [128, 512], F32) for i in range(3)]
        self.small = [self.sb(f"small{i}", [128, 16], F32) for i in range(4)]
        self.wslots = [self.sb(f"wslot{i}", [128, 8, 512], BF16) for i in range(self.NSLOT)]
        self.ring = {"valid": {}, "n": 0, "keys": [], "pos": []}
        self.pb = []
        for i in range(8):
            t = self.gst.enter_context(nc.psum_tensor(f"pb{i}", [128, 512], F32))
            self.pb.append(Tile(t, self.newbuf(f"pb{i}")))
        self.ps_rot = 0
        self.acc_rot = 0
        self.rot = {"sqc": 0, "tmpf": 0, "xu": 0, "small": 0}
        self.stores = []

        self.dma("sp", self.pp.t[:], self.pp_d[:, :], [], [self.pp], self.pp)
        self.dma("pool", self.cst.t[:], self.cst_d[:, :, :], [], [self.cst], self.cst)
        self.dma("sp", self.scS.t[:], self.scT_d[:, :, :], [], [self.scS], self.scS)
        self.op("pool", lambda e: e.memset(self.eps.t[:], EPS), [], [self.eps])
        self.op("pool", lambda e: e.memset(self.dummy.t[:], 0.0), [], [self.dummy])
        self.ident = self.cst.t[:, 0, :]
        self.bones = self.cst.t[:, 1, :]
        self.swapm = self.cst.t[:, 2, :]
        self.ones = self.cst.t[:, 3, :]

        self.preamble_mod()
        if 1 in self.layers:
            self.preamble_na()
        for li in self.layers:
            for s in range(nseq):
                if li == 0:
                    self.attn_layer(li, s, "a")
                elif li == 1:
                    self.attn_layer(li, s, "b")
                elif li == 2:
                    self.lru_layer(li, s)
                else:
                    self.attn_layer(li, s, "d")
                self.mlp(li, s)
        self.P.emit(self.stores)
        return nc

    def ppv(self, name, a=None, b=None):
        off, w = self.lay[name]
        if a is None:
            return self.pp.t[:, off:off + w]
        return self.pp.t[:, off + a:off + b]

    def nxt(self, name):
        lst = getattr(self, name)
        i = self.rot[name]
        self.rot[name] = (i + 1) % len(lst)
        return lst[i]

    def preamble_mod(self):
        self.act(self.scS.t[:], self.scS.t[:], AF.Silu, [self.scS], [self.scS])
        with self.scope() as st:
            wm = [self.sb(f"wm{i}", [128, 8, 768], F32, st) for i in range(2)]
            for i in range(4):
                if i not in self.layers:
                    continue
                bank = self.ps()
                for cb in range(8):
                    wt = wm[cb % 2]
                    src = self.w["w_mod"][i, :, cb * 768:(cb + 1) * 768].rearrange("(kc p) n -> p kc n", p=128)
                    self.dma("sp", wt.t[:, 0:4, :], src[:, 0:4, :], [], [wt], wt)
                    self.dma("sp", wt.t[:, 4:8, :], src[:, 4:8, :], [], [wt], wt)
                    for m in range(6):
                        mm_ = cb * 6 + m
                        for kc in range(8):
                            self.mm(bank.t[:, mm_ * 8:mm_ * 8 + 8], bank, wt.t[:, kc, m * 128:(m + 1) * 128],
                                    self.scS.t[:, kc, :], [wt, self.scS], kc == 0, kc == 7)
                bo, _ = self.lay["bmod"]
                bm = self.pp.t[:, bo + i * 48: bo + (i + 1) * 48].unsqueeze(2).broadcast_to([128, 48, 8])
                self.tt("dve", self.modT.t[:, i, :, :], bank.t[:, 0:384].rearrange("p (m r) -> p m r", r=8), bm,
                        ALU.add, [bank, self.pp], [self.modT])
                for (j, gname) in ((1, "n1g"), (4, "n2g")):
                    go, _ = self.lay[gname]
                    g = self.pp.t[:, go + i * 8: go + (i + 1) * 8].unsqueeze(2).broadcast_to([128, 8, 8])
                    sl = self.modT.t[:, i, j * 8:(j + 1) * 8, :]
                    self.ts("dve", sl, sl, 1.0, None, ALU.add, None, [self.modT], [self.modT])
                    self.tt("dve", sl, sl, g, ALU.mult, [self.modT, self.pp], [self.modT])

    def mv(self, li, j, c, row):
        return self.modT.t[:, li, j * 8 + c, row:row + 1]

    def xsrc(self, s, blk):
        if blk == 0:
            base = self.cxS if self.c_in_s[s] else self.cxT
            return base[s].rearrange("(c p) t -> p c t", p=128)
        base = self.outT if self.x_in_out[s] else self.xT
        return base[s, :, (blk - 1) * 512: blk * 512].rearrange("(c p) t -> p c t", p=128)

    def xdst(self, s, blk):
        if blk == 0:
            return self.cxS[s].rearrange("(c p) t -> p c t", p=128)
        return self.outT[s, :, (blk - 1) * 512: blk * 512].rearrange("(c p) t -> p c t", p=128)

    @staticmethod
    def blk_n(blk):
        return 256 if blk == 0 else 512

    @staticmethod
    def blk_t0(blk):
        return 0 if blk == 0 else 256 + (blk - 1) * 512

    def load_x(self, s, blk):
        n = self.blk_n(blk)
        src = self.xsrc(s, blk)
        self.dma("sp", self.xb.t[:, 0:4, :n], src[:, 0:4, :], [self.Dx[s][blk]], [self.xb], self.xb)
        self.dma("sp", self.xb.t[:, 4:8, :n], src[:, 4:8, :], [self.Dx[s][blk]], [self.xb], self.xb)

    def norm_mod(self, s, blk, li, jB, jA, out_t, out_ap_fn):
        n = self.blk_n(blk)
        row = 4 if blk == 0 else s
        bank = self.ps()
        for c in range(8):
            sq = self.nxt("sqc")
            self.tt("pool", sq.t[:, :n], self.xb.t[:, c, :n], self.xb.t[:, c, :n], ALU.mult, [self.xb], [sq])
            self.mm(bank.t[:, :n], bank, self.ones, sq.t[:, :n], [self.cst, sq], c == 0, c == 7)
        self.act(self.std.t[:, :n], bank.t[:, :n], AF.Sqrt, [bank, self.eps], [self.std], bias=self.eps.t[:, 0:1], scale=1.0 / D)
        self.op("dve", lambda e, o=self.rstd.t[:, :n], i=self.std.t[:, :n]: e.reciprocal(out=o, in_=i), [self.std], [self.rstd])
        for c in range(8):
            tf = self.nxt("tmpf")
            self.stt(tf.t[:, :n], self.xb.t[:, c, :n], self.mv(li, jA, c, row), self.rstd.t[:, :n], ALU.mult, ALU.mult,
                     [self.xb, self.modT, self.rstd], [tf])
            self.act(out_ap_fn(c), tf.t[:, :n], AF.Identity, [tf, self.modT], [out_t], bias=self.mv(li, jB, c, row), scale=1.0)

    def update_x(self, s, blk, co, c0, nn, bank, li, jG):
        row = 4 if blk == 0 else s
        xu = self.nxt("xu")
        src = self.xsrc(s, blk)[:, co, c0:c0 + nn]
        dst = self.xdst(s, blk)[:, co, c0:c0 + nn]
        self.dma("sp", xu.t[:, :nn], src, [self.Dx[s][blk]], [xu], xu)
        self.stt(xu.t[:, :nn], bank.t[:, :nn], self.mv(li, jG, co, row), xu.t[:, :nn], ALU.mult, ALU.add,
                 [bank, self.modT, xu], [xu])
        st = self.dma("sp", dst, xu.t[:, :nn], [xu], [self.Dx2[s][blk]], xu)
        self.stores.append(st)

    def wtile(self, wname, idx, col0, ncols=512, krows=D, k0=0):
        w = self.w[wname] if idx is None else self.w[wname][idx]
        src = w[k0:k0 + 1024, col0:col0 + ncols].rearrange("(kc p) n -> p kc n", p=128)

        def fn(t, src=src, ncols=ncols):
            return [(t[:, 0:4, 0:ncols], src[:, 0:4, :]), (t[:, 4:8, 0:ncols], src[:, 4:8, :])]
        return self.wget((wname, idx, col0, ncols, k0), fn)

    def mlp(self, li, s):
        need_ctx = li < 3
        self.Dx2 = self.Dx
        with self.scope() as st:
            h1 = self.sb("h1", [128, 32, 512], BF16, st)
            for blk in range(0 if need_ctx else 1, 5):
                n = self.blk_n(blk)
                self.load_x(s, blk)
                self.norm_mod(s, blk, li, 3, 4, self.hb, lambda c, n=n: self.hb.t[:, c, :n])
                for j in range(32):
                    wt = self.wtile("w_mlp1", li, (j // 4) * 512)
                    bank = self.ps()
                    for kc in range(8):
                        self.mm(bank.t[:, :n], bank, wt.t[:, kc, (j % 4) * 128:(j % 4 + 1) * 128], self.hb.t[:, kc, :n],
                                [wt, self.hb], kc == 0, kc == 7)
                    tf = self.nxt("tmpf")
                    self.act(tf.t[:, :n], bank.t[:, :n], AF.Relu, [bank], [tf])
                    self.tt("pool", h1.t[:, j, :n], tf.t[:, :n], tf.t[:, :n], ALU.mult, [tf], [h1])
                for nb in range(2):
                    accs = [self.acc() for _ in range(4)]
                    for g in range(4):
                        wt = self.wtile("w_mlp2", li, nb * 512, k0=g * 1024)
                        for q in range(4):
                            for kk in range(8):
                                self.mm(accs[q].t[:, :n], accs[q], wt.t[:, kk, q * 128:(q + 1) * 128], h1.t[:, g * 8 + kk, :n],
                                        [wt, h1], g == 0 and kk == 0, g == 3 and kk == 7)
                    for q in range(4):
                        self.update_x(s, blk, nb * 4 + q, 0, n, accs[q], li, 5)
                if blk == 0:
                    self.c_in_s[s] = True
            self.x_in_out[s] = True

    def qk_post(self, bank, n, gname, gsname, rope_c0, out_ap, out_t, rope):
        qsb = self.nxt("sqc")
        self.act(qsb.t[:, :n], bank.t[:, :n], AF.Copy, [bank], [qsb])
        sq = self.nxt("sqc")
        self.act(sq.t[:, :n], bank.t[:, :n], AF.Square, [bank], [sq])
        bss = self.ps()
        self.mm(bss.t[:, :n], bss, self.bones, sq.t[:, :n], [self.cst, sq], True, True)
        std = self.nxt("tmpf")
        self.act(std.t[:, :n], bss.t[:, :n], AF.Sqrt, [bss, self.eps], [std], bias=self.eps.t[:, 0:1], scale=1.0 / 64)
        self.op("dve", lambda e, o=std.t[:, :n]: e.reciprocal(out=o, in_=o), [std], [std])
        g = self.ppv(gname)
        if not rope:
            self.stt(out_ap, bank.t[:, :n], g[:, 0:1], std.t[:, :n], ALU.mult, ALU.mult, [bank, self.pp, std], [out_t])
            return
        gs = self.ppv(gsname)
        brot = self.ps()
        self.mm(brot.t[:, :n], brot, self.swapm, qsb.t[:, :n], [self.cst, qsb], True, True)
        t1 = self.nxt("tmpf")
        self.stt(t1.t[:, :n], bank.t[:, :n], g[:, 0:1], self.ropeC.t[:, rope_c0:rope_c0 + n], ALU.mult, ALU.mult,
                 [bank, self.pp, self.ropeC], [t1])
        t2 = self.nxt("tmpf")
        self.stt(t2.t[:, :n], brot.t[:, :n], gs[:, 0:1], self.ropeS.t[:, rope_c0:rope_c0 + n], ALU.mult, ALU.mult,
                 [brot, self.pp, self.ropeS], [t2])
        self.tt("pool", t1.t[:, :n], t1.t[:, :n], t2.t[:, :n], ALU.add, [t1, t2], [t1])
        self.tt("dve", out_ap, t1.t[:, :n], std.t[:, :n], ALU.mult, [t1, std], [out_t])

    def attn_layer(self, li, s, kind):
        need_ctx = li < 3
        pre = kind
        wq = pre + "_w_qkv"
        wo = pre + "_w_o"
        rope = kind in ("a", "d")
        if kind == "d":
            nkc = 4
            nvh, dv = 4, 64
            kcol0 = 1024
            vcol0 = 1280
            nvcols = 256
        else:
            nkc = 8
            kcol0 = 1024
            vcol0 = 2048
            nvcols = 1024
            nvh, dv = (8, 128) if kind == "a" else (16, 64)
        dv1 = dv + 1
        self.Dx2 = self.Dx
        with self.scope() as st:
            KT = self.sb("KT", [128, nkc, T], BF16, st)
            VA = self.sb("VA", [128, NKT, nvh, dv1], BF16, st)
            QT = self.sb("QT", [128, 8, 512], BF16, st)
            otm = self.sb("otm", [128, 4, D], BF16, st)
            PTn = 896 if kind == "b" else 512
            PT = [self.sb(f"PT{i}", [128, PTn], BF16, st) for i in range(3)]
            ptr = [0]
            if rope:
                self.ropeC = self.sb("ropeC", [128, NLAT], BF16, st)
                self.ropeS = self.sb("ropeS", [128, NLAT], BF16, st)
                self.dma("pool", self.ropeC.t[:], self.ropeC_d[:, :], [], [self.ropeC], self.ropeC)
                self.dma("pool", self.ropeS.t[:], self.ropeS_d[:, :], [], [self.ropeS], self.ropeS)
            if kind == "b":
                Eb = [self.sb(f"Eb{i}", [128, self.na_nt * 128], BF16, st) for i in range(2)]
            if kind == "a":
                odf = [self.sb(f"odf{i}", [128, 128], F32, st) for i in range(3)]
                o0f = self.sb("o0f", [128, 4, 128], F32, st)
                sg = self.sb("sg", [128, 128], F32, st)
                lam = self.sb("lam", [128, 8], F32, st)
                junk = self.sb("junk", [128, 128], BF16, st)
                lo, _ = self.lay["a_lam"]
                lv = self.pp.t[:, lo:lo + 256]
                t = self.nxt("tmpf")
                self.tt("dve", t.t[:, 0:64], lv[:, 0:64], lv[:, 64:128], ALU.mult, [self.pp], [t])
                self.tt("dve", t.t[:, 64:128], lv[:, 128:192], lv[:, 192:256], ALU.mult, [self.pp], [t])
                self.op("dve", lambda e, o=lam.t[:, 0:2], i=t.t[:, 0:128].rearrange("p (a b) -> p a b", a=2):
                        e.reduce_sum(out=o, in_=i, axis=AX.X), [t], [lam])
                self.act(lam.t[:, 2:4], lam.t[:, 0:2], AF.Exp, [lam], [lam])
                lam_init = 0.8 - 0.6 * math.exp(-0.3 * li)
                self.tt("dve", lam.t[:, 4:5], lam.t[:, 3:4], lam.t[:, 2:3], ALU.subtract, [lam], [lam])
                self.ts("dve", lam.t[:, 4:5], lam.t[:, 4:5], -lam_init, None, ALU.add, None, [lam], [lam])
                self.ts("dve", sg.t[:], self.ppv("a_subln"), 1.0 - lam_init, None, ALU.mult, None, [self.pp], [sg])
            self.op("pool", lambda e, a=VA.t[:, :, :, dv:dv1]: e.memset(a, 1.0), [], [VA])

            def kreq():
                if kind != "d":
                    return None
                src = self.w[wq][:, 1024:1280].rearrange("(kc p) n -> p kc n", p=128)

                def fn(t, src=src):
                    out = []
                    for hk in range(4):
                        for dup in range(2):
                            out.append((t[:, :, hk * 128 + dup * 64: hk * 128 + dup * 64 + 64], src[:, :, hk * 64:(hk + 1) * 64]))
                    return out
                return self.wget((wq, "kdup"), fn)

            for blk in range(5):
                n = self.blk_n(blk)
                t0 = self.blk_t0(blk)
                self.load_x(s, blk)
                self.norm_mod(s, blk, li, 0, 1, self.hb, lambda c, n=n: self.hb.t[:, c, :n])
                for c in range(nkc):
                    if kind == "d":
                        wt = kreq()
                        wcol = c * 128
                    else:
                        wt = self.wtile(wq, None, kcol0 + (c // 4) * 512)
                        wcol = (c % 4) * 128
                    bank = self.ps()
                    for kc in range(8):
                        self.mm(bank.t[:, :n], bank, wt.t[:, kc, wcol:wcol + 128], self.hb.t[:, kc, :n], [wt, self.hb], kc == 0, kc == 7)
                    self.qk_post(bank, n, pre + "_kg", pre + "_kgs", t0 - NCTX, KT.t[:, c, t0:t0 + n], KT, rope and blk > 0)
                for tt_ in range(n // 128):
                    ti = t0 // 128 + tt_
                    for vb in range((nvcols + 511) // 512):
                        ncol = min(512, nvcols - vb * 512)
                        wt = self.wtile(wq, None, vcol0 + vb * 512, ncols=ncol)
                        bank = self.ps()
                        for kc in range(8):
                            self.mm(bank.t[:, :ncol], bank, self.hb.t[:, kc, tt_ * 128:(tt_ + 1) * 128], wt.t[:, kc, :ncol],
                                    [wt, self.hb], kc == 0, kc == 7)
                        nh = ncol // dv
                        h0 = vb * 512 // dv
                        self.act(VA.t[:, ti, h0:h0 + nh, 0:dv], bank.t[:, :ncol].rearrange("p (h d) -> p h d", d=dv), AF.Copy, [bank], [VA])

            for blk in range(0 if need_ctx else 1, 5):
                n = self.blk_n(blk)
                nqt = n // 128
                t0 = self.blk_t0(blk)
                self.load_x(s, blk)
                self.norm_mod(s, blk, li, 0, 1, self.hb, lambda c, n=n: self.hb.t[:, c, :n])
                for c in range(8):
                    wt = self.wtile(wq, None, (c // 4) * 512)
                    bank = self.ps()
                    for kc in range(8):
                        self.mm(bank.t[:, :n], bank, wt.t[:, kc, (c % 4) * 128:(c % 4 + 1) * 128], self.hb.t[:, kc, :n], [wt, self.hb], kc == 0, kc == 7)
                    self.qk_post(bank, n, pre + "_qg", pre + "_qgs", t0 - NCTX, QT.t[:, c, :n], QT, rope and blk > 0)
                kts = list(range(2)) if blk == 0 else list(range(NKT))
                if kind in ("a", "d"):
                    nhm = 16
                    for hm in range(nhm):
                        cq = hm // 2
                        base = (hm % 2) * 64
                        if kind == "d":
                            ck = hm // 4
                            hv = hm // 4
                        else:
                            ck = cq
                            hv = hm // 2
                        accs = [self.acc() for _ in range(nqt)]
                        for ki, kt in enumerate(kts):
                            bs = self.ps()
                            self.mm(bs.t[:, :n], bs, KT.t[base:base + 64, ck, kt * 128:(kt + 1) * 128], QT.t[base:base + 64, cq, :n],
                                    [KT, QT], True, True)
                            pt = PT[ptr[0] % 3]
                            ptr[0] += 1
                            self.act(pt.t[:, :n], bs.t[:, :n], AF.Exp, [bs], [pt], scale=SCALE)
                            for qt in range(nqt):
                                self.mm(accs[qt].t[:, 0:dv1], accs[qt], pt.t[:, qt * 128:(qt + 1) * 128], VA.t[:, kt, hv, :],
                                        [pt, VA], ki == 0, ki == len(kts) - 1)
                        sm = self.nxt("small")
                        if kind == "d":
                            for qt in range(nqt):
                                self.op("dve", lambda e, o=sm.t[:, qt:qt + 1], i=accs[qt].t[:, dv:dv1]: e.reciprocal(out=o, in_=i), [accs[qt]], [sm])
                            for qt in range(nqt):
                                self.act(otm.t[:, qt, hm * 64:(hm + 1) * 64], accs[qt].t[:, 0:dv], AF.Copy, [accs[qt], sm], [otm], scale=sm.t[:, qt:qt + 1])
                        else:
                            m = hm % 2
                            h = hm // 2
                            for qt in range(nqt):
                                self.op("dve", lambda e, o=sm.t[:, qt:qt + 1], i=accs[qt].t[:, dv:dv1]: e.reciprocal(out=o, in_=i), [accs[qt]], [sm])
                            if m == 0:
                                for qt in range(nqt):
                                    self.act(o0f.t[:, qt, :], accs[qt].t[:, 0:dv], AF.Copy, [accs[qt], sm], [o0f], scale=sm.t[:, qt:qt + 1])
                            else:
                                self.ts("dve", sm.t[:, 4:4 + nqt], sm.t[:, 0:nqt], lam.t[:, 4:5], None, ALU.mult, None, [sm, lam], [sm])
                                for qt in range(nqt):
                                    od = odf[qt % 3]
                                    self.stt(od.t[:], accs[qt].t[:, 0:dv], sm.t[:, 4 + qt:5 + qt], o0f.t[:, qt, :], ALU.mult, ALU.add,
                                             [accs[qt], sm, o0f], [od])
                                    self.act(junk.t[:], od.t[:], AF.Square, [od], [junk, sm], accum_out=sm.t[:, 8 + qt:9 + qt])
                                    self.act(sm.t[:, 12 + qt:13 + qt], sm.t[:, 8 + qt:9 + qt], AF.Sqrt, [sm, self.eps], [sm],
                                             bias=self.eps.t[:, 0:1], scale=1.0 / 128)
                                    self.op("dve", lambda e, o=sm.t[:, 12 + qt:13 + qt]: e.reciprocal(out=o, in_=o), [sm], [sm])
                                    self.stt(otm.t[:, qt, h * 128:(h + 1) * 128], od.t[:], sm.t[:, 12 + qt:13 + qt], sg.t[:], ALU.mult, ALU.mult,
                                             [od, sm, sg], [otm])
                else:
                    for h in range(16):
                        cq = h // 2
                        base = (h % 2) * 64
                        if blk > 0:
                            E = Eb[h % 2]
                            self.dma("sp", E.t[:], self.naE[h], [self.DnaE], [E], E)
                        for qt in range(nqt):
                            if blk == 0:
                                tiles = [(0, None), (1, None)]
                            else:
                                g = (blk - 1) * 4 + qt
                                tiles = [(0, None), (1, None)] + [(2 + kl, ty) for (kl, ty) in self.na_plan[g]]
                            pt = PT[ptr[0] % 3]
                            ptr[0] += 1
                            nt_ = len(tiles)
                            banks = [self.ps() for _ in range((nt_ + 3) // 4)]
                            for j, (kt, ty) in enumerate(tiles):
                                bk = banks[j // 4]
                                self.mm(bk.t[:, (j % 4) * 128:(j % 4 + 1) * 128], bk, KT.t[base:base + 64, cq, kt * 128:(kt + 1) * 128],
                                        QT.t[base:base + 64, cq, qt * 128:(qt + 1) * 128], [KT, QT], True, True)
                            for bi, bk in enumerate(banks):
                                w_ = min(4, nt_ - bi * 4) * 128
                                self.act(pt.t[:, bi * 512: bi * 512 + w_], bk.t[:, :w_], AF.Exp, [bk], [pt], scale=SCALE)
                            if blk > 0:
                                tys = [ty for (_, ty) in tiles[2:]]
                                if tys == [0, 1, 2, 3, 4]:
                                    self.tt("dve", pt.t[:, 256:896], pt.t[:, 256:896], E.t[:, 0:640], ALU.mult, [pt, E], [pt])
                                else:
                                    for j, ty in enumerate(tys):
                                        self.tt("dve", pt.t[:, 256 + j * 128: 384 + j * 128], pt.t[:, 256 + j * 128: 384 + j * 128],
                                                E.t[:, ty * 128:(ty + 1) * 128], ALU.mult, [pt, E], [pt])
                            ac = self.acc()
                            for j, (kt, ty) in enumerate(tiles):
                                self.mm(ac.t[:, 0:dv1], ac, pt.t[:, j * 128:(j + 1) * 128], VA.t[:, kt, h, :], [pt, VA], j == 0, j == nt_ - 1)
                            sm = self.nxt("small")
                            self.op("dve", lambda e, o=sm.t[:, 0:1], i=ac.t[:, dv:dv1]: e.reciprocal(out=o, in_=i), [ac], [sm])
                            self.act(otm.t[:, qt, h * 64:(h + 1) * 64], ac.t[:, 0:dv], AF.Copy, [ac, sm], [otm], scale=sm.t[:, 0:1])
                for c in range(8):
                    bank = self.ps()
                    bv = bank.t[:].bitcast(BF16)
                    for qt in range(nqt):
                        self.op("pe", lambda e, o=bv[:, qt * 128:(qt + 1) * 128], i=otm.t[:, qt, c * 128:(c + 1) * 128], idn=self.ident:
                                e.transpose(o, i, idn), [otm, self.cst], [bank])
                    self.op("dve", lambda e, o=QT.t[:, c, :n], i=bv[:, :n]: e.tensor_copy(out=o, in_=i), [bank], [QT])
                for co in range(8):
                    wt = self.wtile(wo, None, (co // 4) * 512)
                    bank = self.ps()
                    for c in range(8):
                        self.mm(bank.t[:, :n], bank, wt.t[:, c, (co % 4) * 128:(co % 4 + 1) * 128], QT.t[:, c, :n], [wt, QT], c == 0, c == 7)
                    self.update_x(s, blk, co, 0, n, bank, li, 2)
                if blk == 0:
                    self.c_in_s[s] = True
            self.x_in_out[s] = True

    def preamble_na(self):
        with self.scope() as st:
            nt = self.na_nt
            bt = [self.sb(f"nab{i}", [128, nt * 128], F32, st) for i in range(2)]
            mk = self.sb("nam", [128, nt * 128], F32, st)
            eo = [self.sb(f"nae{i}", [128, nt * 128], BF16, st) for i in range(2)]
            self.dma("sp", mk.t[:], self.nam_d.rearrange("p t q -> p (t q)"), [], [mk], mk)
            for h in range(16):
                b = bt[h % 2]
                o = eo[h % 2]
                self.dma("sp", b.t[:], self.nab_d[h].rearrange("p t q -> p (t q)"), [], [b], b)
                self.act(b.t[:], b.t[:], AF.Exp, [b], [b])
                self.tt("dve", o.t[:], b.t[:], mk.t[:], ALU.mult, [b, mk], [o])
                self.dma("sp", self.naE[h], o.t[:], [o], [self.DnaE], o)

    def lru_layer(self, li, s):
        need_ctx = li < 3
        self.Dx2 = self.Dx
        segs = [(0, NCTX), (NCTX, T)]
        ranges = [(0, 256)] + [(256 + 512 * j, 512) for j in range(4)]
        with self.scope() as st:
            hT = self.sb("hT", [128, 8, T], BF16, st)
            zst = self.sb("zst", [128, T], BF16, st)
            U0x = Tile(self.xb.t[:].rearrange("p c n -> p (c n)"), self.xb.b)
            U1 = [self.sb(f"U1{i}", [128, T], F32, st) for i in range(2)]
            Ub = [self.sb(f"Ub{i}", [128, T], BF16, st) for i in range(2)]
            Rr = self.sb("Rr", [128, T], F32, st)
            Ii = self.sb("Ii", [128, T], F32, st)
            HF = self.sb("HF", [128, T], F32, st)
            HB = self.sb("HB", [128, T], F32, st)
            Mm = HB
            nsp = self.sb("nsp", [128, 2, 8], F32, st)
            spt = [self.sb(f"spt{i}", [128, 16], F32, st) for i in range(4)]
            for d_, nm in enumerate(("c_fwd_lam", "c_bwd_lam")):
                self.act(spt[0].t[:, d_ * 8:(d_ + 1) * 8], self.ppv(nm), AF.Exp, [self.pp], [spt[0]], scale=-1.0)
            xx = spt[0].t[:, 0:16]
            self.ts("dve", spt[1].t[:], xx, -0.25, 1.0 / 3.0, ALU.mult, ALU.add, [spt[0]], [spt[1]])
            self.tt("dve", spt[1].t[:], spt[1].t[:], xx, ALU.mult, [spt[1], spt[0]], [spt[1]])
            self.ts("dve", spt[1].t[:], spt[1].t[:], -1.0, 0.5, ALU.mult, ALU.add, [spt[1]], [spt[1]])
            self.tt("dve", spt[1].t[:], spt[1].t[:], xx, ALU.mult, [spt[1], spt[0]], [spt[1]])
            self.ts("dve", spt[1].t[:], spt[1].t[:], -1.0, 1.0, ALU.mult, ALU.add, [spt[1]], [spt[1]])
            self.tt("dve", spt[1].t[:], spt[1].t[:], xx, ALU.mult, [spt[1], spt[0]], [spt[1]])
            self.act(spt[2].t[:], xx, AF.Ln, [spt[0]], [spt[2]], bias=1.0, scale=1.0)
            self.ts("dve", spt[3].t[:], xx, 0.05, None, ALU.is_lt, None, [spt[0]], [spt[3]])
            self.tt("dve", spt[1].t[:], spt[1].t[:], spt[2].t[:], ALU.subtract, [spt[1], spt[2]], [spt[1]])
            self.tt("dve", spt[1].t[:], spt[1].t[:], spt[3].t[:], ALU.mult, [spt[1], spt[3]], [spt[1]])
            self.tt("dve", spt[1].t[:], spt[1].t[:], spt[2].t[:], ALU.add, [spt[1], spt[2]], [spt[1]])
            self.ts("dve", nsp.t[:].rearrange("p a b -> p (a b)"), spt[1].t[:], -8.0, None, ALU.mult, None, [spt[1]], [nsp])

            for blk in range(5):
                n = self.blk_n(blk)
                t0 = self.blk_t0(blk)
                self.load_x(s, blk)
                self.norm_mod(s, blk, li, 0, 1, hT, lambda c, n=n, t0=t0: hT.t[:, c, t0:t0 + n])
            cwo, _ = self.lay["c_convw"]
            cbo, _ = self.lay["c_convb"]
            for nb in range(4):
                for cc in range(2):
                    c = nb * 2 + cc
                    for (r0, rn) in ranges:
                        wt = self.wtile("c_w_in", None, 1024 + (c // 4) * 512)
                        bank = self.ps()
                        for kc in range(8):
                            self.mm(bank.t[:, :rn], bank, wt.t[:, kc, (c % 4) * 128:(c % 4 + 1) * 128], hT.t[:, kc, r0:r0 + rn], [wt, hT], kc == 0, kc == 7)
                        self.act(U0x.t[:, r0:r0 + rn], bank.t[:, :rn], AF.Copy, [bank], [U0x])
                    wv = lambda j, c=c: self.pp.t[:, cwo + j * 8 + c: cwo + j * 8 + c + 1]
                    cb = self.pp.t[:, cbo + c: cbo + c + 1]
                    for (a, b) in segs:
                        self.ts("dve", U1[cc].t[:, a:b], U0x.t[:, a:b], wv(2), cb, ALU.mult, ALU.add, [U0x, self.pp], [U1[cc]])
                        self.stt(U1[cc].t[:, a + 2:b], U0x.t[:, a:b - 2], wv(0), U1[cc].t[:, a + 2:b], ALU.mult, ALU.add, [U0x, self.pp, U1[cc]], [U1[cc]])
                        self.stt(U1[cc].t[:, a + 1:b], U0x.t[:, a:b - 1], wv(1), U1[cc].t[:, a + 1:b], ALU.mult, ALU.add, [U0x, self.pp, U1[cc]], [U1[cc]])
                        self.stt(U1[cc].t[:, a:b - 1], U0x.t[:, a + 1:b], wv(3), U1[cc].t[:, a:b - 1], ALU.mult, ALU.add, [U0x, self.pp, U1[cc]], [U1[cc]])
                    self.act(Ub[cc].t[:], U1[cc].t[:], AF.Copy, [U1[cc]], [Ub[cc]])
                for cc in range(2):
                    c = nb * 2 + cc
                    for d_ in range(2):
                        def fn(t, d_=d_, nb=nb):
                            wa = self.w["c_gw"][d_ * 2 + 0, nb].rearrange("(kc p) n -> p kc n", p=128)
                            wx = self.w["c_gw"][d_ * 2 + 1, nb].rearrange("(kc p) n -> p kc n", p=128)
                            return [(t[:, 0:2, 0:256], wa), (t[:, 2:4, 0:256], wx)]
                        wt = self.wget(("c_gw", d_, nb), fn)
                        ba = self.ppv("c_fwd_b_a" if d_ == 0 else "c_bwd_b_a")[:, c:c + 1]
                        bx = self.ppv("c_fwd_b_x" if d_ == 0 else "c_bwd_b_x")[:, c:c + 1]
                        for (r0, rn) in ranges:
                            for gi, (dst, bb_) in enumerate(((Rr, ba), (Ii, bx))):
                                bank = self.ps()
                                for k2 in range(2):
                                    self.mm(bank.t[:, :rn], bank, wt.t[:, gi * 2 + k2, cc * 128:(cc + 1) * 128], Ub[k2].t[:, r0:r0 + rn],
                                            [wt, Ub[k2]], k2 == 0, k2 == 1)
                                self.act(dst.t[:, r0:r0 + rn], bank.t[:, :rn], AF.Sigmoid, [bank, self.pp], [dst], bias=bb_, scale=1.0)
                        self.act(Rr.t[:], Rr.t[:], AF.Exp, [Rr, nsp], [Rr], scale=nsp.t[:, d_, c:c + 1])
                        self.tt("pool", Mm.t[:], Rr.t[:], Rr.t[:], ALU.mult, [Rr], [Mm])
                        self.ts("dve", Mm.t[:], Mm.t[:], -1.0, 1.0, ALU.mult, ALU.add, [Mm], [Mm])
                        self.ts("dve", Mm.t[:], Mm.t[:], 0.0, None, ALU.max, None, [Mm], [Mm])
                        self.act(Mm.t[:], Mm.t[:], AF.Sqrt, [Mm], [Mm])
                        self.tt("pool", Ii.t[:], Ii.t[:], U1[cc].t[:], ALU.mult, [Ii, U1[cc]], [Ii])
                        self.tt("dve", Ii.t[:], Ii.t[:], Mm.t[:], ALU.mult, [Ii, Mm], [Ii])
                        if d_ == 0:
                            self.op("dve", lambda e, o=HF.t[:], a=Rr.t[:], b=Ii.t[:]:
                                    e.tensor_tensor_scan(out=o, data0=a, data1=b, initial=0.0, op0=ALU.mult, op1=ALU.add), [Rr, Ii], [HF])
                        else:
                            self.op("dve", lambda e, o=HB.t[:, 0:NCTX][:, ::-1], a=Rr.t[:, 0:NCTX][:, ::-1], b=Ii.t[:, 0:NCTX][:, ::-1]:
                                    e.tensor_tensor_scan(out=o, data0=a, data1=b, initial=0.0, op0=ALU.mult, op1=ALU.add), [Rr, Ii], [HB])
                            self.op("dve", lambda e, o=HB.t[:, NCTX:T][:, ::-1], a=Rr.t[:, NCTX:T][:, ::-1], b=Ii.t[:, NCTX:T][:, ::-1], ini=HB.t[:, 0:1]:
                                    e.tensor_tensor_scan(out=o, data0=a, data1=b, initial=ini, op0=ALU.mult, op1=ALU.add), [Rr, Ii, HB], [HB])
                    self.tt("pool", HF.t[:], HF.t[:], HB.t[:], ALU.add, [HF, HB], [HF])
                    for (r0, rn) in ranges:
                        wt = self.wtile("c_w_in", None, (c // 4) * 512)
                        bank = self.ps()
                        for kc in range(8):
                            self.mm(bank.t[:, :rn], bank, wt.t[:, kc, (c % 4) * 128:(c % 4 + 1) * 128], hT.t[:, kc, r0:r0 + rn], [wt, hT], kc == 0, kc == 7)
                        self.act(HB.t[:, r0:r0 + rn], bank.t[:, :rn], AF.Gelu_apprx_tanh, [bank], [HB])
                    self.tt("dve", zst.t[:], HF.t[:], HB.t[:], ALU.mult, [HF, HB], [zst])
                    self.dma("sp", self.zS[s, c * 128:(c + 1) * 128, :], zst.t[:], [zst], [self.DzS[s]], zst)
            for blk in range(0 if need_ctx else 1, 5):
                n = self.blk_n(blk)
                t0 = self.blk_t0(blk)
                zsrc = self.zS[s, :, t0:t0 + n].rearrange("(c p) t -> p c t", p=128)
                self.dma("sp", self.hb.t[:, :, :n], zsrc, [self.DzS[s]], [self.hb], self.hb)
                for co in range(8):
                    wt = self.wtile("c_w_o", None, (co // 4) * 512)
                    bank = self.ps()
                    for c in range(8):
                        self.mm(bank.t[:, :n], bank, wt.t[:, c, (co % 4) * 128:(co % 4 + 1) * 128], self.hb.t[:, c, :n], [wt, self.hb], c == 0, c == 7)
                    self.update_x(s, blk, co, 0, n, bank, li, 2)
                if blk == 0:
                    self.c_in_s[s] = True
            self.x_in_out[s] = True


def run_cores(inputs, nseq=4, cores=8, layers=(0, 1, 2, 3)):
    inp = {k: np.asarray(v) for k, v in inputs.items()}
    pp, lay = build_pp(inp)
    plan, nt, nab, nam = na_tables(inp["b_rpb"][0])
    rC, rS = rope_tables()
    cst = consts_arr()
    shared = {
        "pp": pp, "cst": cst, "ropeC": rC, "ropeS": rS, "nabias": nab, "namask": nam,
        "w_mod": np.ascontiguousarray(inp["w_mod"], np.float32),
        "w_mlp1": np.ascontiguousarray(inp["w_mlp1"], np.float32),
        "w_mlp2": np.ascontiguousarray(inp["w_mlp2"], np.float32),
        "a_w_qkv": np.ascontiguousarray(inp["a_w_qkv"][0]), "a_w_o": np.ascontiguousarray(inp["a_w_o"][0]),
        "b_w_qkv": np.ascontiguousarray(inp["b_w_qkv"][0]), "b_w_o": np.ascontiguousarray(inp["b_w_o"][0]),
        "c_w_in": np.ascontiguousarray(inp["c_w_in"][0]), "c_w_o": np.ascontiguousarray(inp["c_w_o"][0]),
        "c_gw": np.ascontiguousarray(np.stack([inp["c_fwd_w_a"][0], inp["c_fwd_w_x"][0], inp["c_bwd_w_a"][0], inp["c_bwd_w_x"][0]], axis=0)),
        "d_w_qkv": np.ascontiguousarray(inp["d_w_qkv"][0]), "d_w_o": np.ascontiguousarray(inp["d_w_o"][0]),
    }
    b = Builder(nseq, tuple(layers), lay, nt, plan)
    nc = b.build()
    in_maps = []
    for ci in range(cores):
        b0 = ci * nseq
        m = dict(shared)
        m["xT"] = np.ascontiguousarray(inp["x"][b0:b0 + nseq].transpose(0, 2, 1))
        m["cxT"] = np.ascontiguousarray(inp["ctx"][b0:b0 + nseq].transpose(0, 2, 1))
        sc = np.zeros((128, 8, 8), np.float32)
        sc[:, :, :nseq] = inp["c"][b0:b0 + nseq].reshape(nseq, 8, 128).transpose(2, 1, 0)
        sc[:, :, 4] = inp["c_ctx"].reshape(8, 128).T
        m["scT"] = sc
        in_maps.append(m)
    res = run_bass_kernel_spmd(nc, in_maps, core_ids=list(range(cores)))
    outs = [np.asarray(r["outT"]).transpose(0, 2, 1) for r in res.results]
    return np.ascontiguousarray(np.concatenate(outs, axis=0).astype(np.float32))


def kernel(**inputs):
    return run_cores(inputs, nseq=4, cores=8, layers=(0, 1, 2, 3))
```
